# Optimizing a Trainium2 kernel written in Bass

```python
import math
import numpy as np
import jax
import jax.numpy as jnp
from jax import lax

D_MODEL = 1024
BATCH = 4
SEQ = 4096
DEPTH = 2
DEC_BATCH = 32
DEC_SEQ = 8
PAST_LEN = 8192
PAGE_SIZE = 128

N_MIXERS = 2
N_HGRN_LAYERS = (DEPTH + N_MIXERS - 1) // N_MIXERS
N_ATTN_LAYERS = DEPTH // N_MIXERS
HG_EXPAND = 128
HG_HEADS = D_MODEL // HG_EXPAND
HG_DK = HG_EXPAND
HG_DV = D_MODEL // HG_HEADS
HG_WIDTH = HG_HEADS * HG_DK
HG_CHUNK = 64
GROUPS = ((128, 1), (512, 4), (2048, 16))
N_GROUPS = len(GROUPS)
ATT_HEAD_DIM = 64
ATT_HEADS = D_MODEL // ATT_HEAD_DIM
ATT_WIDTH = ATT_HEADS * ATT_HEAD_DIM
ATT_SCALE = ATT_HEAD_DIM ** -0.5
D_FF = 4 * D_MODEL
RMS_EPS = 1e-6

kernel_name = 'hybrid_hgrn2_dilated_swa_step'


def _rmsnorm(x, w):
    x32 = x.astype(jnp.float32)
    y = x32 * lax.rsqrt(jnp.mean(x32 * x32, axis=-1, keepdims=True) + RMS_EPS) * w.astype(jnp.float32)
    return y.astype(x.dtype)


def _alibi_slopes():
    n = N_GROUPS * ATT_HEADS
    s = 2.0 ** (-8.0 * np.arange(1, n + 1) / n)
    return jnp.asarray(s.reshape(N_GROUPS, ATT_HEADS), dtype=jnp.float32)


def _sqrelu_mlp(x, w_up, w_down):
    h = jax.nn.relu(x @ w_up)
    return (h * h) @ w_down


def _hgrn_recurrence(q, k, v, logf, s0):
    b_, t_, h_, _ = q.shape
    dv = v.shape[-1]
    c = math.gcd(t_, HG_CHUNK)
    n = t_ // c

    def chunks(a):
        return a.astype(jnp.float32).reshape(b_, n, c, h_, a.shape[-1]).transpose(1, 0, 3, 2, 4)

    causal = jnp.tril(jnp.ones((c, c), dtype=bool))

    def step(s, inp):
        qc, kc, vc, gc = inp
        cum = jnp.cumsum(gc, axis=2)
        diff = cum[:, :, :, None, :] - cum[:, :, None, :, :]
        decay = jnp.exp(jnp.where(causal[:, :, None], diff, -jnp.inf))
        attn = jnp.einsum('bhtk,bhsk,bhtsk->bhts', qc, kc, decay)
        o = jnp.einsum('bhts,bhsv->bhtv', attn, vc) + jnp.einsum('bhtk,bhkv->bhtv', qc * jnp.exp(cum), s)
        last = cum[:, :, -1:, :]
        s_new = jnp.exp(last[:, :, 0, :, None]) * s + jnp.einsum('bhsk,bhsv->bhkv', kc * jnp.exp(last - cum), vc)
        return s_new, o

    s_t, o = lax.scan(step, s0.astype(jnp.float32), (chunks(q), chunks(k), chunks(v), chunks(logf)))
    o = o.transpose(1, 0, 3, 2, 4).reshape(b_, t_, h_, dv)
    return o, s_t


def _hgrn_mixer(xn, s0, lb, w_q, w_f, w_i, w_g, w_o, w_gnorm):
    b_, t_, _ = xn.shape

    def heads(a):
        return a.reshape(b_, t_, HG_HEADS, -1)

    q = jax.nn.silu(xn @ w_q)
    f = lb + (1.0 - lb) * jax.nn.sigmoid((xn @ w_f).astype(jnp.float32))
    k = 1.0 - f
    i = xn @ w_i
    o, s_t = _hgrn_recurrence(heads(q), heads(k), heads(i), heads(jnp.log(f)), s0)
    o = _rmsnorm(o.reshape(b_, t_, HG_WIDTH), w_gnorm) * jax.nn.silu((xn @ w_g).astype(jnp.float32))
    return o.astype(xn.dtype) @ w_o, s_t


def _qkv(xn, w_qkv, w_qn, w_kn):
    b_, t_, _ = xn.shape
    qkv = (xn @ w_qkv).reshape(b_, t_, 3, N_GROUPS, ATT_HEADS, ATT_HEAD_DIM)
    q = _rmsnorm(qkv[:, :, 0], w_qn)
    k = _rmsnorm(qkv[:, :, 1], w_kn)
    return q, k, qkv[:, :, 2]


def _banded_window(q, k, v, n_back, slopes):
    n_, l_, h_, dh = q.shape
    bq = n_back
    nb = -(-l_ // bq)
    pad = nb * bq - l_
    qb = jnp.pad(q, ((0, 0), (0, pad), (0, 0), (0, 0))).reshape(n_, nb, bq, h_, dh)

    def key_blocks(a):
        ap = jnp.pad(a, ((0, 0), (bq, pad), (0, 0), (0, 0)))
        prev = ap[:, :nb * bq].reshape(n_, nb, bq, h_, dh)
        cur = ap[:, bq:].reshape(n_, nb, bq, h_, dh)
        return jnp.concatenate([prev, cur], axis=2)

    kb, vb = key_blocks(k), key_blocks(v)
    dist = jnp.arange(bq)[:, None] - jnp.arange(2 * bq)[None, :] + bq
    band = (dist >= 0) & (dist <= n_back)
    before_start = (jnp.arange(nb)[:, None, None] == 0) & (jnp.arange(2 * bq)[None, None, :] < bq)
    valid = band[None] & ~before_start
    s = jnp.einsum('nbqhd,nbkhd->nbhqk', qb, kb, preferred_element_type=jnp.float32) * ATT_SCALE
    s = s - slopes[:, None, None] * dist.astype(jnp.float32)
    s = jnp.where(valid[:, None], s, -jnp.inf)
    lse = jax.nn.logsumexp(s, axis=-1)
    p = jnp.exp(s - lse[..., None])
    o = jnp.einsum('nbhqk,nbkhd->nbqhd', p, vb.astype(jnp.float32))
    o = o.reshape(n_, nb * bq, h_, dh)[:, :l_]
    lse = lse.transpose(0, 1, 3, 2).reshape(n_, nb * bq, h_)[:, :l_]
    return o, lse


def _dilated_prompt(q, k, v, window, dilation, slopes):
    b_, t_, h_, dh = q.shape
    l_ = t_ // dilation

    def to_sub(a):
        return a.reshape(b_, l_, dilation, h_, dh).transpose(0, 2, 1, 3, 4).reshape(b_ * dilation, l_, h_, dh)

    o, lse = _banded_window(to_sub(q), to_sub(k), to_sub(v), window // dilation, slopes * dilation)
    o = o.reshape(b_, dilation, l_, h_, dh).transpose(0, 2, 1, 3, 4).reshape(b_, t_, h_, dh)
    lse = lse.reshape(b_, dilation, l_, h_).transpose(0, 2, 1, 3).reshape(b_, t_, h_)
    return o, lse


def _dilated_sample(q, k_all, v_all, window, dilation, slopes):
    s_new = q.shape[1]
    first = k_all.shape[1] - s_new
    steps = jnp.arange(window // dilation + 1)
    idx = first + jnp.arange(s_new)[:, None] - dilation * steps[None, :]
    valid = idx >= 0
    idx = jnp.maximum(idx, 0)
    kg = k_all[:, idx]
    vg = v_all[:, idx]
    s = jnp.einsum('bshd,bskhd->bhsk', q, kg, preferred_element_type=jnp.float32) * ATT_SCALE
    s = s - slopes[None, :, None, None] * (dilation * steps).astype(jnp.float32)
    s = jnp.where(valid[None, None], s, -jnp.inf)
    lse = jax.nn.logsumexp(s, axis=-1)
    p = jnp.exp(s - lse[..., None])
    o = jnp.einsum('bhsk,bskhd->bshd', p, vg.astype(jnp.float32))
    return o, lse.transpose(0, 2, 1)


def _merge_groups(outs, lses):
    wts = jax.nn.softmax(jnp.stack(lses, axis=0), axis=0)
    o = jnp.einsum('gbth,gbthd->bthd', wts, jnp.stack(outs, axis=0))
    return o.reshape(o.shape[0], o.shape[1], ATT_WIDTH)


def _attn_prompt(xn, w_qkv, w_o, w_qn, w_kn):
    q, k, v = _qkv(xn, w_qkv, w_qn, w_kn)
    slopes = _alibi_slopes()
    t_ = xn.shape[1]
    outs, lses, rows = [], [], []
    for g, (window, dilation) in enumerate(GROUPS):
        o, lse = _dilated_prompt(q[:, :, g], k[:, :, g], v[:, :, g], window, dilation, slopes[g])
        outs.append(o)
        lses.append(lse)
        keep = min(window, t_)
        rows.append(jnp.stack([k[:, t_ - keep:, g], v[:, t_ - keep:, g]], axis=2))
    return _merge_groups(outs, lses).astype(xn.dtype) @ w_o, rows


def _attn_sample(xn, bufs, w_qkv, w_o, w_qn, w_kn):
    q, k, v = _qkv(xn, w_qkv, w_qn, w_kn)
    slopes = _alibi_slopes()
    outs, lses, rows = [], [], []
    for g, (window, dilation) in enumerate(GROUPS):
        buf = bufs[g]
        k_all = jnp.concatenate([buf[:, :, 0].astype(k.dtype), k[:, :, g]], axis=1)
        v_all = jnp.concatenate([buf[:, :, 1].astype(v.dtype), v[:, :, g]], axis=1)
        o, lse = _dilated_sample(q[:, :, g], k_all, v_all, window, dilation, slopes[g])
        outs.append(o)
        lses.append(lse)
        rows.append(jnp.stack([k[:, :, g], v[:, :, g]], axis=2))
    return _merge_groups(outs, lses).astype(xn.dtype) @ w_o, rows


def setup_inputs(seed: int = 0) -> dict:
    key = jax.random.key(seed)
    ks = jax.random.split(key, 24)
    f32 = jnp.float32

    def dense(k, shape, fan_in):
        return jax.random.normal(k, shape, f32) * (fan_in ** -0.5)

    def gain(k, shape):
        return 1.0 + 0.05 * jax.random.normal(k, shape, f32)

    def kv_cache(k, window):
        return jax.random.normal(k, (N_ATTN_LAYERS, DEC_BATCH, min(window, PAST_LEN), 2, ATT_HEADS, ATT_HEAD_DIM), f32)

    return {
        'x_prompt': jax.random.normal(ks[0], (BATCH, SEQ, D_MODEL), f32),
        'x_sample': jax.random.normal(ks[1], (DEC_BATCH, DEC_SEQ, D_MODEL), f32),
        'state_hgrn': jax.random.normal(ks[2], (N_HGRN_LAYERS, DEC_BATCH, HG_HEADS, HG_DK, HG_DV), f32),
        'cache_kv_w128': kv_cache(ks[3], GROUPS[0][0]),
        'cache_kv_w512': kv_cache(ks[4], GROUPS[1][0]),
        'cache_kv_w2048': kv_cache(ks[5], GROUPS[2][0]),
        'hg_lb_logits': 0.1 * jax.random.normal(ks[6], (DEPTH + 1, HG_WIDTH), f32),
        'hg_w_q': dense(ks[7], (N_HGRN_LAYERS, D_MODEL, HG_WIDTH), D_MODEL),
        'hg_w_f': dense(ks[8], (N_HGRN_LAYERS, D_MODEL, HG_WIDTH), D_MODEL),
        'hg_w_i': dense(ks[9], (N_HGRN_LAYERS, D_MODEL, HG_HEADS * HG_DV), D_MODEL),
        'hg_w_g': dense(ks[10], (N_HGRN_LAYERS, D_MODEL, HG_HEADS * HG_DV), D_MODEL),
        'hg_w_o': dense(ks[11], (N_HGRN_LAYERS, HG_HEADS * HG_DV, D_MODEL), HG_HEADS * HG_DV),
        'hg_norm_o': gain(ks[12], (N_HGRN_LAYERS, HG_HEADS * HG_DV)),
        'att_w_qkv': dense(ks[13], (N_ATTN_LAYERS, D_MODEL, 3 * N_GROUPS * ATT_WIDTH), D_MODEL),
        'att_w_o': dense(ks[14], (N_ATTN_LAYERS, ATT_WIDTH, D_MODEL), ATT_WIDTH),
        'att_q_norm': gain(ks[15], (N_ATTN_LAYERS, ATT_HEAD_DIM)),
        'att_k_norm': gain(ks[16], (N_ATTN_LAYERS, ATT_HEAD_DIM)),
        'norm_mix': gain(ks[17], (DEPTH, D_MODEL)),
        'norm_ffn': gain(ks[18], (DEPTH, D_MODEL)),
        'ffn_w_up': dense(ks[19], (DEPTH, D_MODEL, D_FF), D_MODEL),
        'ffn_w_down': dense(ks[20], (DEPTH, D_FF, D_MODEL), D_FF),
    }


def reference(x_prompt, x_sample, state_hgrn, cache_kv_w128, cache_kv_w512, cache_kv_w2048,
              hg_lb_logits, hg_w_q, hg_w_f, hg_w_i, hg_w_g, hg_w_o, hg_norm_o,
              att_w_qkv, att_w_o, att_q_norm, att_k_norm,
              norm_mix, norm_ffn, ffn_w_up, ffn_w_down):
    lb_all = jnp.cumsum(jax.nn.softmax(hg_lb_logits.astype(jnp.float32), axis=0), axis=0)
    yp, ys = x_prompt, x_sample
    hg_p, hg_s = [], []
    kv_p = [[] for _ in GROUPS]
    kv_s = [[] for _ in GROUPS]
    for layer in range(DEPTH):
        a = layer // N_MIXERS
        xnp = _rmsnorm(yp, norm_mix[layer])
        xns = _rmsnorm(ys, norm_mix[layer])
        if layer % N_MIXERS == 0:
            lw = (lb_all[layer], hg_w_q[a], hg_w_f[a], hg_w_i[a], hg_w_g[a], hg_w_o[a], hg_norm_o[a])
            s0p = jnp.zeros((yp.shape[0], HG_HEADS, HG_DK, HG_DV), jnp.float32)
            mp, sp = _hgrn_mixer(xnp, s0p, *lw)
            ms, ss = _hgrn_mixer(xns, state_hgrn[a], *lw)
            hg_p.append(sp)
            hg_s.append(ss)
        else:
            aw = (att_w_qkv[a], att_w_o[a], att_q_norm[a], att_k_norm[a])
            mp, rows_p = _attn_prompt(xnp, *aw)
            ms, rows_s = _attn_sample(xns, (cache_kv_w128[a], cache_kv_w512[a], cache_kv_w2048[a]), *aw)
            for g in range(N_GROUPS):
                kv_p[g].append(rows_p[g])
                kv_s[g].append(rows_s[g])
        yp = yp + mp.astype(yp.dtype)
        ys = ys + ms.astype(ys.dtype)
        yp = yp + _sqrelu_mlp(_rmsnorm(yp, norm_ffn[layer]), ffn_w_up[layer], ffn_w_down[layer]).astype(yp.dtype)
        ys = ys + _sqrelu_mlp(_rmsnorm(ys, norm_ffn[layer]), ffn_w_up[layer], ffn_w_down[layer]).astype(ys.dtype)
    return (yp, ys, jnp.stack(hg_p), jnp.stack(hg_s),
            jnp.stack(kv_p[0]), jnp.stack(kv_s[0]),
            jnp.stack(kv_p[1]), jnp.stack(kv_s[1]),
            jnp.stack(kv_p[2]), jnp.stack(kv_s[2]))
```

```python
import numpy as np
from contextlib import ExitStack
import concourse.bass as bass
import concourse.mybir as mybir
from concourse.bass_utils import run_bass_kernel_spmd

F32 = mybir.dt.float32
BF16 = mybir.dt.bfloat16
AF = mybir.ActivationFunctionType
ALU = mybir.AluOpType
AX = mybir.AxisListType

GROUPS = ((128, 1), (512, 4), (2048, 16))
EPS = 1e-6


class Buf:
    __slots__ = ("name", "ap", "last_w", "readers")

    def __init__(self, name, ap):
        self.name = name
        self.ap = ap
        self.last_w = None
        self.readers = []

    def __getitem__(self, idx):
        return self.ap[idx]


class DmaGroup:
    def __init__(self, name):
        self.name = name
        self.cnt = 0


class Prog:
    ENGS = ("sync", "scalar", "vector", "gpsimd", "tensor")

    def __init__(self, nc, stack):
        self.nc = nc
        self.stack = stack
        self.q = {e: [] for e in self.ENGS}
        self.seq = {e: 0 for e in self.ENGS}
        self.waited = {e: {} for e in self.ENGS}
        self.sems = {}
        self.groups = []
        self.targets = {e: set() for e in self.ENGS}
        self.gmap = {}
        for e in self.ENGS:
            self.sems["p_" + e] = stack.enter_context(nc.semaphore("p_" + e))

    def sbuf(self, name, shape, dtype):
        return self.stack.enter_context(self.nc.sbuf_tensor("s_" + name, list(shape), dtype)).ap()

    def psum(self, name, shape, dtype=F32):
        return self.stack.enter_context(self.nc.psum_tensor("ps_" + name, list(shape), dtype)).ap()

    def group(self, name):
        g = DmaGroup(name)
        self.sems["d_" + name] = self.stack.enter_context(self.nc.semaphore("d_" + name))
        self.groups.append(g)
        self.gmap["d_" + name] = g
        return g

    def _deps(self, eng, reads, writes):
        need = {}

        def add(tok):
            if tok is not None and need.get(tok[0], 0) < tok[1]:
                need[tok[0]] = tok[1]
        for b in reads:
            add(b.last_w)
        for b in writes:
            add(b.last_w)
            for r in b.readers:
                add(r)
        waits = []
        w = self.waited[eng]
        for k, v in need.items():
            if k == "p_tensor" and eng == "tensor":
                continue
            if k.startswith("d_"):
                v = self.gmap[k].cnt
            if w.get(k, 0) < v:
                w[k] = v
                waits.append((k, v))
                if k.startswith("p_"):
                    self.targets[k[2:]].add(v)
        return waits

    def _commit(self, tok, reads, writes):
        for b in reads:
            b.readers.append(tok)
            if len(b.readers) > 64:
                m = {}
                for k, v in b.readers:
                    if m.get(k, 0) < v:
                        m[k] = v
                b.readers = list(m.items())
        for b in writes:
            b.last_w = tok
            b.readers = []

    def op(self, eng, fn, reads=(), writes=()):
        waits = self._deps(eng, reads, writes)
        self.seq[eng] += 1
        tok = ("p_" + eng, self.seq[eng])
        self.q[eng].append((fn, waits, "p_" + eng, self.seq[eng]))
        self._commit(tok, reads, writes)
        return tok

    def dma(self, eng, fn, grp, reads=(), writes=(), after=()):
        waits = self._deps(eng, reads, list(writes) + list(after))
        grp.cnt += 16
        tok = ("d_" + grp.name, grp.cnt)
        self.q[eng].append((fn, waits, "d_" + grp.name, None))
        self._commit(tok, reads, writes)
        return tok

    def emit(self):
        toks = [("d_" + g.name, g.cnt) for g in self.groups if g.cnt]
        for e in self.ENGS:
            if self.seq[e]:
                toks.append(("p_" + e, self.seq[e]))
                self.targets[e].add(self.seq[e])
        self.q["sync"].append((None, toks, None, None))
        rank = {}
        for e in self.ENGS:
            rank[e] = {idx: r + 1 for r, idx in enumerate(sorted(self.targets[e]))}
        with self.nc.Block() as block:
            def run(engname):
                def body(e):
                    for fn, waits, inc, own in self.q[engname]:
                        for k, v in waits:
                            if k.startswith("p_"):
                                v = rank[k[2:]][v]
                            e.wait_ge(self.sems[k], v)
                        if fn is not None:
                            ins = fn(e)
                            if own is None:
                                ins.then_inc(self.sems[inc], 16)
                            elif own in rank[engname]:
                                ins.then_inc(self.sems[inc], 1)
                return body
            block.sync(run("sync"))
            block.scalar(run("scalar"))
            block.vector(run("vector"))
            block.gpsimd(run("gpsimd"))
            block.tensor(run("tensor"))


def build_nc(stage=99):
    nc = bass.Bass("TRN2", target_bir_lowering=False)

    def din(name, shape):
        return nc.dram_tensor(name, list(shape), F32, kind="ExternalInput").ap()

    def dout(name, shape):
        return nc.dram_tensor(name, list(shape), F32, kind="ExternalOutput").ap()

    xs = din("xs", [4096, 1024])
    xsm = din("xsm", [32, 1024])
    st_in = din("st_in", [4, 8, 128, 128])
    caches = [din("c%d" % g, [4, GROUPS[g][0], 2, 1024]) for g in range(3)]
    lbl = din("lbl", [3, 1024])
    hw = {k: din("hg_" + k, [1024, 1024]) for k in ("q", "f", "i", "g", "o")}
    gno = din("gno", [1, 1024])
    wqkv = din("wqkv", [1024, 9216])
    awo = din("awo", [1024, 1024])
    qn_w = din("qn_w", [1, 64])
    kn_w = din("kn_w", [1, 64])
    nmix = din("nmix", [2, 1024])
    nffn = din("nffn", [2, 1024])
    wup = din("wup", [2, 1024, 4096])
    wdn = din("wdn", [2, 4096, 1024])
    idn = din("idn", [128, 128])
    hmask_d = din("hmask", [128, 128])
    hmask8_d = din("hmask8", [32, 32])
    rst_d = din("rst", [2, 1024])
    flag_d = din("flag", [128, 1])
    mtab_d = din("mtab", [128, 48 * 256])
    bmask_d = din("bmask", [32, 4])
    stab_d = din("stab", [128, 13 * 8 * 16])
    sntab_d = din("sntab", [32, 3 * 16 * 32])
    selq_d = din("selq", [32, 32 * 128])
    selk_d = din("selk", [128, 32 * 32])

    yp = dout("yp", [2048, 1024])
    ys = dout("ys", [32, 1024])
    st_p = dout("st_p", [8, 128, 128])
    st_s = dout("st_s", [4, 8, 128, 128])
    kvp = [dout("kvp%d" % g, [GROUPS[g][0], 2, 1024]) for g in range(3)]
    kvs = [dout("kvs%d" % g, [32, 2, 1024]) for g in range(3)]

    ybuf = nc.dram_tensor("ybuf", [4096 + 32, 1024], F32, kind="Internal").ap()
    accd = [nc.dram_tensor("accd%d" % g, [2048, 1040], F32, kind="Internal").ap() for g in range(3)]
    skvd = [nc.dram_tensor("skvd%d" % g, [32, 2, 1024], F32, kind="Internal").ap() for g in range(3)]
    sqd = [nc.dram_tensor("sqd%d" % g, [32, 1024], F32, kind="Internal").ap() for g in range(3)]
    skvB = [Buf("skvB%d" % g, skvd[g]) for g in range(3)]

    with ExitStack() as st:
        P = Prog(nc, st)
        V = lambda fn, r=(), w=(): P.op("vector", fn, r, w)
        A = lambda fn, r=(), w=(): P.op("scalar", fn, r, w)
        G = lambda fn, r=(), w=(): P.op("gpsimd", fn, r, w)
        T = lambda fn, r=(), w=(): P.op("tensor", fn, r, w)

        g_const = P.group("const")
        g_wrep = P.group("wrep")
        g_c3 = P.group("c3")
        g_w = P.group("w")
        g_in = [P.group("in%d" % i) for i in range(3)]
        g_out = [P.group("out%d" % i) for i in range(3)]
        g_misc = P.group("misc")
        g_acc = [P.group("acc%d" % i) for i in range(3)]
        g_ast = [P.group("ast%d" % i) for i in range(2)]
        g_kv = [P.group("kv%d" % i) for i in range(2)]
        g_cache = [P.group("cache%d" % i) for i in range(2)]

        ytile = [Buf("ybuf%d" % t, ybuf) for t in range(33)]
        accb = [Buf("accd%d" % g, accd[g]) for g in range(3)]

        arena = P.sbuf("arena", [128, 65536], BF16)
        arenaB = Buf("arena", arena)
        WB = [Buf("WB%d" % i, arena) for i in range(5)]
        GW = [P.group("gw%d" % i) for i in range(5)]

        def f32s(name, shape):
            return Buf(name, P.sbuf(name, shape, F32))

        def b16s(name, shape):
            return Buf(name, P.sbuf(name, shape, BF16))

        ident = b16s("ident", [128, 128])
        ones_bf = b16s("ones_bf", [128, 128])
        epsb = f32s("epsb", [128, 1])
        flag = f32s("flag", [128, 1])
        hmask = b16s("hmask", [128, 128])
        hmask8 = b16s("hmask8", [32, 32])
        wrep = [f32s("wrep0", [128, 1024])] * 2
        xin = [f32s("xin%d" % i, [128, 1024]) for i in range(3)]
        yo = xin
        xn = [b16s("xn%d" % i, [128, 1024]) for i in range(2)]
        xnT = [b16s("xnT%d" % i, [128, 8, 128]) for i in range(2)]
        ssb = [f32s("ssb%d" % i, [128, 1]) for i in range(2)]
        tf = [f32s("tf%d" % i, [128, 1024]) for i in range(6)]
        tb = [b16s("tb%d" % i, [128, 1024]) for i in range(8)]
        small = [f32s("small%d" % i, [128, 16]) for i in range(4)]
        tbx = [b16s("tbx%d" % i, [128, 1024]) for i in range(4)]
        gate2 = f32s("gate2", [128, 1024])
        decs = [f32s("dec%d" % i, [128, 32]) for i in range(2)]
        rstdB = f32s("rstdB", [128, 128])

        pb = [Buf("pb%d" % i, P.psum("pb%d" % i, [128, 512], F32)) for i in range(7)]
        pT = Buf("pT", P.psum("pT", [128, 1024], BF16))

        P.dma("gpsimd", lambda e: e.dma_start(out=ident[:], in_=idn[:, :]), g_w, writes=[ident])
        P.dma("gpsimd", lambda e: e.dma_start(out=hmask[:], in_=hmask_d[:, :]), g_w, writes=[hmask])
        P.dma("gpsimd", lambda e: e.dma_start(out=hmask8[:], in_=hmask8_d[:, :]), g_w, writes=[hmask8])
        P.dma("sync", lambda e: e.dma_start(out=flag[:], in_=flag_d[:, :]), g_const, writes=[flag])
        V(lambda e: e.memset(epsb[:], EPS), w=[epsb])
        V(lambda e: e.memset(ones_bf[:], 1.0), w=[ones_bf])

        def load_w(dst_ap3, src2d, nk, ncols, col0=0, wi=0, extra=()):
            for kc in range(nk):
                for c0 in range(0, ncols, 2048):
                    cw = min(2048, ncols - c0)
                    P.dma("gpsimd", lambda e, kc=kc, c0=c0, cw=cw: e.dma_start(
                        out=dst_ap3[:, kc, c0:c0 + cw],
                        in_=src2d[kc * 128:(kc + 1) * 128, col0 + c0:col0 + c0 + cw]),
                        GW[wi], writes=[WB[wi]], after=[arenaB] + list(extra))

        def load_norm_w(slot, src_row):
            P.dma("sync", lambda e: e.dma_start(out=wrep[slot][:], in_=src_row.partition_broadcast(128)),
                  g_wrep, writes=[wrep[slot]])

        cnt = {"in": 0, "out": 0}

        def load_x(src_ap, Tn, src_bufs, i):
            xi = xin[i]
            P.dma("sync", lambda e: e.dma_start(out=xi[:Tn, :], in_=src_ap), g_in[i], reads=src_bufs, writes=[xi])

        def norm_T(src_ap, Tn, wslot, src_bufs, i, preloaded=False, xs=None):
            xs = i if xs is None else xs
            xi, xnb, xt, sb = xin[xs], xn[i], xnT[i], ssb[i]
            if not preloaded:
                load_x(src_ap, Tn, src_bufs, xs)
            jk = xnb
            A(lambda e: e.activation(out=jk[:Tn, :], in_=xi[:Tn, :], func=AF.Square, accum_out=sb[:Tn, :]),
              [xi], [jk, sb])
            A(lambda e: e.activation(out=sb[:Tn, :], in_=sb[:Tn, :], func=AF.Ln, scale=1.0 / 1024, bias=epsb[:Tn, 0:1]),
              [sb, epsb], [sb])
            A(lambda e: e.activation(out=sb[:Tn, :], in_=sb[:Tn, :], func=AF.Exp, scale=-0.5), [sb], [sb])
            V(lambda e: e.scalar_tensor_tensor(out=xnb[:Tn, :], in0=xi[:Tn, :], scalar=sb[:Tn, 0:1], in1=wrep[wslot][:Tn, :],
                                               op0=ALU.mult, op1=ALU.mult), [xi, sb, wrep[wslot]], [xnb])
            for c in range(8):
                T(lambda e, c=c: e.transpose(out=pT[:, c * 128:c * 128 + Tn], in_=xnb[:Tn, c * 128:(c + 1) * 128],
                                             identity=ident[:Tn, :Tn]), [xnb, ident], [pT])
            A(lambda e: e.copy(out=xt[:, :, :Tn], in_=pT.ap.rearrange("p (c t) -> p c t", c=8)[:, :, :Tn]), [pT], [xt])
            return xi, xt

        def store_rows(dst_ap, src_buf, Tn, i, dst_bufs):
            P.dma("sync", lambda e: e.dma_start(out=dst_ap, in_=src_buf[:Tn, :]), g_out[i], reads=[src_buf], writes=dst_bufs)

        def v3(buf, Tn):
            return buf.ap.rearrange("p (h t) -> p h t", h=8)[:, :, :Tn]

        Wq = arena[:, 0:8192].rearrange("p (k n) -> p k n", k=8)
        Wf = arena[:, 8192:16384].rearrange("p (k n) -> p k n", k=8)
        Wi = arena[:, 16384:24576].rearrange("p (k n) -> p k n", k=8)
        Wg = arena[:, 24576:32768].rearrange("p (k n) -> p k n", k=8)
        Wo = arena[:, 32768:40960].rearrange("p (k n) -> p k n", k=8)
        for wi_, (k_, W_) in enumerate((("q", Wq), ("f", Wf), ("g", Wg), ("i", Wi), ("o", Wo))):
            load_w(W_, hw[k_], 8, 1024, wi=wi_)
        load_norm_w(0, nmix[0:1, :])
        ov = arena[:, 40960:65536].bitcast(F32)
        ovb = [Buf("ov%d" % i, ov[:, i * 1024:(i + 1) * 1024]) for i in range(12)]
        lbT, omlT, wgT, rstT, rstT8, Sst, gate, oT = ovb[0:8]
        lb3 = ovb[8]
        lbv = lbl.rearrange("l (h k) -> k l h", h=8)
        lraw = Buf("lraw", lb3.ap[:, 0:24].rearrange("p (l h) -> p l h", l=3))
        P.dma("sync", lambda e: e.dma_start(out=lraw[:], in_=lbv, allow_slow_non_contiguous=True), g_const, writes=[lb3])
        P.dma("sync", lambda e: e.dma_start(out=lb3[:, 48:56], in_=gno.rearrange("o (h v) -> v (o h)", h=8), allow_slow_non_contiguous=True), g_const, writes=[lb3])
        P.dma("sync", lambda e: e.dma_start(out=rstT[:], in_=rst_d[0:1, :].partition_broadcast(128)), g_const, writes=[rstT])
        P.dma("sync", lambda e: e.dma_start(out=rstT8[:], in_=rst_d[1:2, :].partition_broadcast(128)), g_const, writes=[rstT8])
        bmask = f32s("bmask", [32, 4])
        P.dma("sync", lambda e: e.dma_start(out=bmask[:], in_=bmask_d[:, :]), g_const, writes=[bmask])
        A(lambda e: e.activation(out=lb3[:, 0:24], in_=lb3[:, 0:24], func=AF.Exp), [lb3], [lb3])
        V(lambda e: e.tensor_tensor(out=lb3[:, 24:32], in0=lb3[:, 0:8], in1=lb3[:, 8:16], op=ALU.add), [lb3], [lb3])
        V(lambda e: e.tensor_tensor(out=lb3[:, 24:32], in0=lb3[:, 24:32], in1=lb3[:, 16:24], op=ALU.add), [lb3], [lb3])
        V(lambda e: e.reciprocal(out=lb3[:, 24:32], in_=lb3[:, 24:32]), [lb3], [lb3])
        V(lambda e: e.tensor_tensor(out=lb3[:, 32:40], in0=lb3[:, 0:8], in1=lb3[:, 24:32], op=ALU.mult), [lb3], [lb3])
        V(lambda e: e.tensor_scalar(out=lb3[:, 40:48], in0=lb3[:, 32:40], scalar1=-1.0, scalar2=1.0, op0=ALU.mult, op1=ALU.add),
          [lb3], [lb3])
        V(lambda e: e.tensor_copy(out=v3(lbT, 128), in_=lb3[:, 32:40].unsqueeze(2).to_broadcast([128, 8, 128])), [lb3], [lbT])
        V(lambda e: e.tensor_copy(out=v3(omlT, 128), in_=lb3[:, 40:48].unsqueeze(2).to_broadcast([128, 8, 128])), [lb3], [omlT])
        V(lambda e: e.tensor_copy(out=v3(wgT, 128), in_=lb3[:, 48:56].unsqueeze(2).to_broadcast([128, 8, 128])), [lb3], [wgT])
        V(lambda e: e.memset(Sst[:], 0.0), w=[Sst])
        SbfA, SbfB = tbx[2], tbx[3]
        V(lambda e: e.memset(SbfA[:], 0.0), w=[SbfA])

        def hgrn_A(ti, src_ap, Tn, sample, pre=None):
            i = ti % 2
            C = 8 if sample else 64
            nch = Tn // C
            xi, xt = pre if pre is not None else norm_T(src_ap, Tn, 0, [], i, xs=ti % 3)
            sg, qf, ff, cum, Aex, kE = tf
            qp, kp, kppT, vb = tb[0 + i], tb[2 + i], tb[4 + i], tb[6 + i]
            kpp = tbx[0]
            gate = (ovb[6], gate2)[i]
            dec = decs[i]
            rs = rstT8 if sample else rstT
            yield
            for h in range(8):
                for kc in range(8):
                    T(lambda e, h=h, kc=kc: e.matmul(pb[h // 4][:, (h % 4) * 128:(h % 4) * 128 + Tn], lhsT=Wq[:, kc, h * 128:(h + 1) * 128],
                                                     rhs=xt[:, kc, :Tn], start=kc == 0, stop=kc == 7), [xt, arenaB, WB[0]], [pb[h // 4]])
            for h in range(8):
                for kc in range(8):
                    T(lambda e, h=h, kc=kc: e.matmul(pb[2 + h // 4][:, (h % 4) * 128:(h % 4) * 128 + Tn], lhsT=Wf[:, kc, h * 128:(h + 1) * 128],
                                                     rhs=xt[:, kc, :Tn], start=kc == 0, stop=kc == 7), [xt, arenaB, WB[1]], [pb[2 + h // 4]])

            yield

            def p3(j, Tn=Tn):
                return pb[j].ap.rearrange("p (h t) -> p h t", h=4)[:, :, :Tn]

            def s3(buf, half, Tn=Tn):
                return buf.ap.rearrange("p (h t) -> p h t", h=8)[:, half * 4:(half + 1) * 4, :Tn]
            for half in range(2):
                A(lambda e, half=half: e.activation(out=s3(sg, half), in_=p3(half), func=AF.Sigmoid), [pb[half]], [sg])
                V(lambda e, half=half: e.tensor_tensor(out=s3(qf, half), in0=p3(half), in1=s3(sg, half), op=ALU.mult), [pb[half], sg], [qf])
            for half in range(2):
                A(lambda e, half=half: e.activation(out=s3(ff, half), in_=p3(2 + half), func=AF.Sigmoid), [pb[2 + half]], [ff])
            V(lambda e: e.tensor_tensor(out=v3(ff, Tn), in0=v3(ff, Tn), in1=v3(omlT, Tn), op=ALU.mult), [ff, omlT], [ff])
            V(lambda e: e.tensor_tensor(out=v3(ff, Tn), in0=v3(ff, Tn), in1=v3(lbT, Tn), op=ALU.add), [ff, lbT], [ff])
            yield
            A(lambda e: e.activation(out=v3(cum, Tn), in_=v3(ff, Tn), func=AF.Ln), [ff], [cum])
            if not sample:
                V(lambda e: e.tensor_tensor_scan(out=cum[:, :], data0=rs[:, :], data1=cum[:, :], initial=0.0, op0=ALU.mult, op1=ALU.add),
                  [cum, rs], [cum])
            else:
                for h in range(8):
                    V(lambda e, h=h: e.tensor_tensor_scan(out=cum[:, h * 128:h * 128 + Tn], data0=rs[:, h * 128:h * 128 + Tn],
                                                          data1=cum[:, h * 128:h * 128 + Tn], initial=0.0, op0=ALU.mult, op1=ALU.add),
                      [cum, rs], [cum])
            yield
            A(lambda e: e.activation(out=v3(Aex, Tn), in_=v3(cum, Tn), func=AF.Exp), [cum], [Aex])
            A(lambda e: e.activation(out=v3(kE, Tn), in_=v3(cum, Tn), func=AF.Exp, scale=-1.0), [cum], [kE])
            yield
            V(lambda e: e.tensor_scalar(out=v3(ff, Tn), in0=v3(ff, Tn), scalar1=-1.0, scalar2=1.0, op0=ALU.mult, op1=ALU.add), [ff], [ff])
            G(lambda e: e.tensor_tensor(out=v3(qp, Tn), in0=v3(qf, Tn), in1=v3(Aex, Tn), op=ALU.mult), [qf, Aex], [qp])
            V(lambda e: e.tensor_tensor(out=v3(kE, Tn), in0=v3(kE, Tn), in1=v3(ff, Tn), op=ALU.mult), [kE, ff], [kE])
            A(lambda e: e.copy(out=v3(kp, Tn), in_=v3(kE, Tn)), [kE], [kp])
            a4 = Aex.ap.rearrange("p (h t) -> p h t", h=8)[:, :, :Tn].rearrange("p h (n c) -> p h n c", c=C)
            V(lambda e: e.tensor_tensor(out=kpp.ap.rearrange("p (h t) -> p h t", h=8)[:, :, :Tn].rearrange("p h (n c) -> p h n c", c=C),
                                        in0=kE.ap.rearrange("p (h t) -> p h t", h=8)[:, :, :Tn].rearrange("p h (n c) -> p h n c", c=C),
                                        in1=a4[:, :, :, C - 1:C].to_broadcast([128, 8, nch, C]), op=ALU.mult), [kE, Aex], [kpp])
            V(lambda e: e.tensor_copy(out=dec.ap.rearrange("p (h n) -> p h n", h=8)[:, :, :nch], in_=a4[:, :, :, C - 1]), [Aex], [dec])
            yield
            for h in range(8):
                for kc in range(8):
                    T(lambda e, h=h, kc=kc: e.matmul(pb[h // 4][:, (h % 4) * 128:(h % 4) * 128 + Tn], lhsT=Wg[:, kc, h * 128:(h + 1) * 128],
                                                     rhs=xt[:, kc, :Tn], start=kc == 0, stop=kc == 7), [xt, arenaB, WB[2]], [pb[h // 4]])
            for half in range(2):
                for kc in range(8):
                    T(lambda e, half=half, kc=kc: e.matmul(pb[2 + half][:Tn, :], lhsT=xt[:, kc, :Tn], rhs=Wi[:, kc, half * 512:(half + 1) * 512],
                                                           start=kc == 0, stop=kc == 7), [xt, arenaB, WB[3]], [pb[2 + half]])
            for half in range(2):
                A(lambda e, half=half: e.activation(out=s3(sg, half), in_=p3(half), func=AF.Sigmoid), [pb[half]], [sg])
                V(lambda e, half=half: e.tensor_tensor(out=s3(gate, half), in0=p3(half), in1=s3(sg, half), op=ALU.mult), [pb[half], sg], [gate])
                A(lambda e, half=half: e.copy(out=vb[:Tn, half * 512:(half + 1) * 512], in_=pb[2 + half][:Tn, :]), [pb[2 + half]], [vb])
            G(lambda e: e.tensor_tensor(out=v3(gate, Tn), in0=v3(gate, Tn), in1=v3(wgT, Tn), op=ALU.mult), [gate, wgT], [gate])
            yield
            for h in range(8):
                T(lambda e, h=h: e.transpose(out=pT[:Tn, h * 128:(h + 1) * 128], in_=kpp[:, h * 128:h * 128 + Tn], identity=ident[:, :]),
                  [kpp, ident], [pT])
            A(lambda e: e.copy(out=kppT[:Tn, :], in_=pT[:Tn, :]), [pT], [kppT])
        def hgrn_B(ti, Tn, sample):
            i = ti % 2
            xi = xin[ti % 3]
            qp, kp, kppT, vb = tb[0 + i], tb[2 + i], tb[4 + i], tb[6 + i]
            attnm = tbx[1]
            gate = (ovb[6], gate2)[i]
            dec = decs[i]
            hm = hmask8 if sample else hmask

            def s3(buf, half, Tn=Tn):
                return buf.ap.rearrange("p (h t) -> p h t", h=8)[:, half * 4:(half + 1) * 4, :Tn]
            for hg in range(2):
                hs = range(hg * 4, hg * 4 + 4)
                pa, po, pS = pb[4], pb[5], pb[6]
                for h in hs:
                    T(lambda e, h=h: e.matmul(pa[:Tn, (h % 4) * 128:(h % 4) * 128 + Tn], lhsT=kp[:, h * 128:h * 128 + Tn],
                                              rhs=qp[:, h * 128:h * 128 + Tn], start=True, stop=True), [kp, qp], [pa])
                V(lambda e, hg=hg: e.tensor_tensor(out=attnm.ap.rearrange("p (h t) -> p h t", h=8)[:Tn, hg * 4:hg * 4 + 4, :Tn],
                                                   in0=pa.ap.rearrange("p (h t) -> p h t", h=4)[:Tn, :, :Tn],
                                                   in1=hm[:Tn, :Tn].unsqueeze(1).to_broadcast([Tn, 4, Tn]), op=ALU.mult), [pa, hm], [attnm])
                yield
                if not sample:
                    for h in hs:
                        T(lambda e, h=h: e.matmul(pS[:, (h % 4) * 128:(h % 4 + 1) * 128], lhsT=kppT[0:64, h * 128:(h + 1) * 128],
                                                  rhs=vb[0:64, h * 128:(h + 1) * 128], start=True, stop=True), [kppT, vb], [pS])
                    for h in hs:
                        V(lambda e, h=h: e.scalar_tensor_tensor(out=Sst[:, h * 128:(h + 1) * 128], in0=Sst[:, h * 128:(h + 1) * 128],
                                                                scalar=dec[:, h * 4:h * 4 + 1], in1=pS[:, (h % 4) * 128:(h % 4 + 1) * 128],
                                                                op0=ALU.mult, op1=ALU.add), [Sst, dec, pS], [Sst])
                    A(lambda e, hg=hg: e.copy(out=SbfB[:, hg * 512:(hg + 1) * 512], in_=Sst[:, hg * 512:(hg + 1) * 512]), [Sst], [SbfB])
                    for h in hs:
                        o_ = po[:, (h % 4) * 128:(h % 4) * 128 + 128]
                        T(lambda e, h=h, o_=o_: e.matmul(o_, lhsT=vb[:, h * 128:(h + 1) * 128], rhs=attnm[:, h * 128:(h + 1) * 128],
                                                         start=True, stop=False), [vb, attnm], [po])
                        T(lambda e, h=h, o_=o_: e.matmul(o_[:, 0:64], lhsT=SbfA[:, h * 128:(h + 1) * 128], rhs=qp[:, h * 128:h * 128 + 64],
                                                         start=False, stop=False), [SbfA, qp], [po])
                        T(lambda e, h=h, o_=o_: e.matmul(o_[:, 64:128], lhsT=SbfB[:, h * 128:(h + 1) * 128], rhs=qp[:, h * 128 + 64:h * 128 + 128],
                                                         start=False, stop=True), [SbfB, qp], [po])
                    yield
                    for h in hs:
                        T(lambda e, h=h: e.matmul(pS[:, (h % 4) * 128:(h % 4 + 1) * 128], lhsT=kppT[64:128, h * 128:(h + 1) * 128],
                                                  rhs=vb[64:128, h * 128:(h + 1) * 128], start=True, stop=True), [kppT, vb], [pS])
                    for h in hs:
                        V(lambda e, h=h: e.scalar_tensor_tensor(out=Sst[:, h * 128:(h + 1) * 128], in0=Sst[:, h * 128:(h + 1) * 128],
                                                                scalar=dec[:, h * 4 + 1:h * 4 + 2], in1=pS[:, (h % 4) * 128:(h % 4 + 1) * 128],
                                                                op0=ALU.mult, op1=ALU.add), [Sst, dec, pS], [Sst])
                    A(lambda e, hg=hg: e.copy(out=SbfA[:, hg * 512:(hg + 1) * 512], in_=Sst[:, hg * 512:(hg + 1) * 512]), [Sst], [SbfA])
                else:
                    po2 = pb[0]
                    for h in hs:
                        o_ = po[:, (h % 4) * 128:(h % 4) * 128 + 32]
                        T(lambda e, h=h, o_=o_: e.matmul(o_, lhsT=vb[:32, h * 128:(h + 1) * 128], rhs=attnm[:32, h * 128:h * 128 + 32],
                                                         start=True, stop=True), [vb, attnm], [po])
                    for b in range(4):
                        Sbf = SbfA if b % 2 == 0 else SbfB
                        G(lambda e, b=b, Sbf=Sbf, hg=hg: e.tensor_copy(out=Sbf[:, hg * 512:(hg + 1) * 512], in_=sS[b][:, hg * 512:(hg + 1) * 512]),
                          [sS[b]], [Sbf])
                        for h in hs:
                            T(lambda e, h=h, b=b, Sbf=Sbf: e.matmul(po2[:, (h % 4) * 128 + 8 * b:(h % 4) * 128 + 8 * b + 8],
                                                                    lhsT=Sbf[:, h * 128:(h + 1) * 128],
                                                                    rhs=qp[:, h * 128 + 8 * b:h * 128 + 8 * b + 8], start=True, stop=True),
                              [Sbf, qp], [po2])
                    for b in range(4):
                        vm = xn[1 - i]
                        V(lambda e, b=b, vm=vm, hg=hg: e.tensor_scalar(out=vm[:32, hg * 512:(hg + 1) * 512], in0=vb[:32, hg * 512:(hg + 1) * 512],
                                                                       scalar1=bmask[:32, b:b + 1], scalar2=None, op0=ALU.mult), [vb, bmask], [vm])
                        for h in hs:
                            T(lambda e, h=h, b=b, vm=vm: e.matmul(pS[:, (h % 4) * 128:(h % 4 + 1) * 128], lhsT=kppT[0:32, h * 128:(h + 1) * 128],
                                                                  rhs=vm[0:32, h * 128:(h + 1) * 128], start=True, stop=True), [kppT, vm], [pS])
                        for h in hs:
                            V(lambda e, h=h, b=b: e.scalar_tensor_tensor(out=sS[b][:, h * 128:(h + 1) * 128], in0=sS[b][:, h * 128:(h + 1) * 128],
                                                                         scalar=dec[:, h * 4 + b:h * 4 + b + 1],
                                                                         in1=pS[:, (h % 4) * 128:(h % 4 + 1) * 128], op0=ALU.mult, op1=ALU.add),
                              [sS[b], dec, pS], [sS[b]])
                A(lambda e, hg=hg: e.copy(out=s3(oT, hg), in_=po.ap.rearrange("p (h t) -> p h t", h=4)[:, :, :Tn]), [po], [oT])
                if sample:
                    V(lambda e, hg=hg: e.tensor_tensor(out=s3(oT, hg), in0=s3(oT, hg), in1=pb[0].ap.rearrange("p (h t) -> p h t", h=4)[:, :, :Tn],
                                                       op=ALU.add), [oT, pb[0]], [oT])
                yield
            sqo = SbfB
            A(lambda e: e.activation(out=v3(sqo, Tn), in_=v3(oT, Tn), func=AF.Square), [oT], [sqo])
            pn = pb[6]
            for h in range(8):
                T(lambda e, h=h: e.matmul(pn[:, :Tn], lhsT=ones_bf[:, :], rhs=sqo[:, h * 128:h * 128 + Tn], start=h == 0, stop=h == 7),
                  [sqo, ones_bf], [pn])
            rstd = rstdB
            A(lambda e: e.activation(out=rstd[:, :Tn], in_=pn[:, :Tn], func=AF.Ln, scale=1.0 / 1024, bias=epsb[:, 0:1]), [pn, epsb], [rstd])
            A(lambda e: e.activation(out=rstd[:, :Tn], in_=rstd[:, :Tn], func=AF.Exp, scale=-0.5), [rstd], [rstd])
            V(lambda e: e.tensor_tensor(out=v3(oT, Tn), in0=v3(oT, Tn), in1=v3(gate, Tn), op=ALU.mult), [oT, gate], [oT])
            onT = attnm
            V(lambda e: e.tensor_tensor(out=v3(onT, Tn), in0=v3(oT, Tn), in1=rstd[:, :Tn].unsqueeze(1).to_broadcast([128, 8, Tn]), op=ALU.mult),
              [oT, rstd], [onT])
            yield
            for half in range(2):
                for h in range(8):
                    T(lambda e, half=half, h=h: e.matmul(pb[4 + half][:Tn, :], lhsT=onT[:, h * 128:h * 128 + Tn], rhs=Wo[:, h, half * 512:(half + 1) * 512],
                                                         start=h == 0, stop=h == 7), [onT, arenaB, WB[4]], [pb[4 + half]])
            for half in range(2):
                V(lambda e, half=half: e.tensor_tensor(out=xi[:Tn, half * 512:(half + 1) * 512], in0=xi[:Tn, half * 512:(half + 1) * 512],
                                                       in1=pb[4 + half][:Tn, :], op=ALU.add), [xi, pb[4 + half]], [xi])
            dst = ybuf[4096:4128, :] if sample else ybuf[ti * 128:(ti + 1) * 128, :]
            store_rows(dst, xi, Tn, ti % 3, [ytile[ti]])

        def run2(ga, gb, mid_at=None, mid=None):
            da, db = ga is None, gb is None
            n_ = 0
            while not (da and db):
                if mid is not None and n_ == mid_at:
                    mid()
                    mid = None
                n_ += 1
                if not da:
                    try:
                        next(ga)
                    except StopIteration:
                        da = True
                if not db:
                    try:
                        next(gb)
                    except StopIteration:
                        db = True
            if mid is not None:
                mid()

        sS = [ovb[8 + b] for b in range(4)]
        for b in range(4):
            P.dma("sync", lambda e, b=b: e.dma_start(out=sS[b].ap.rearrange("p (h v) -> p h v", h=8), in_=st_in[b].rearrange("h k v -> k h v")),
                  g_misc, writes=[sS[b]])
        n_ptiles = 32 if stage >= 1 else 2
        load_x(xs[0:128, :], 128, [], 0)
        load_x(xs[128:256, :], 128, [], 1)
        hpre = {0: norm_T(None, 128, 0, [], 0, preloaded=True, xs=0)}
        run2(hgrn_A(0, None, 128, False, pre=hpre.pop(0)), None)
        hpre[1] = norm_T(None, 128, 0, [], 1, preloaded=True, xs=1)
        for ti in range(n_ptiles):
            if ti + 2 < n_ptiles:
                load_x(xs[(ti + 2) * 128:(ti + 3) * 128, :], 128, [], (ti + 2) % 3)
            if ti + 1 < n_ptiles:
                ga = hgrn_A(ti + 1, None, 128, False, pre=hpre.pop(ti + 1))
            else:
                ga = hgrn_A(32, xsm[:, :], 32, True)

            def mid(ti=ti):
                if ti + 2 < n_ptiles:
                    hpre[ti + 2] = norm_T(None, 128, 0, [], (ti + 2) % 2, preloaded=True, xs=(ti + 2) % 3)
            run2(ga, hgrn_B(ti, 128, False), mid_at=4, mid=mid)
        P.dma("sync", lambda e: e.dma_start(out=st_p.rearrange("h k v -> k h v"), in_=Sst.ap.rearrange("p (h v) -> p h v", h=8)),
              g_misc, reads=[Sst])
        run2(None, hgrn_B(32, 32, True))
        for b in range(4):
            P.dma("sync", lambda e, b=b: e.dma_start(out=st_s[b].rearrange("h k v -> k h v"), in_=sS[b].ap.rearrange("p (h v) -> p h v", h=8)),
                  g_misc, reads=[sS[b]])

        Wu = arena[:, 0:32768].rearrange("p (k n) -> p k n", k=8)
        Wd = arena[:, 32768:65536].rearrange("p (k n) -> p k n", k=32)
        allov = list(ovb)

        def load_mlp_w(layer):
            for kc in range(8):
                for c0 in range(0, 4096, 2048):
                    P.dma("gpsimd", lambda e, kc=kc, c0=c0: e.dma_start(out=Wu[:, kc, c0:c0 + 2048], in_=wup[layer, kc * 128:(kc + 1) * 128, c0:c0 + 2048]),
                          GW[0], writes=[WB[0]], after=[arenaB] + allov)
            for kc in range(32):
                P.dma("gpsimd", lambda e, kc=kc: e.dma_start(out=Wd[:, kc, :], in_=wdn[layer, kc * 128:(kc + 1) * 128, :]), GW[1], writes=[WB[1]], after=[arenaB] + allov)

        def mlp_pass(jobs):
            pre = {}
            pre[0] = norm_T(jobs[0][0], jobs[0][4], 0, jobs[0][1], 0)
            for ti, job in enumerate(jobs):
                if ti + 1 < len(jobs):
                    nj = jobs[ti + 1]
                    pre[ti + 1] = norm_T(nj[0], nj[4], 0, nj[1], (ti + 1) % 2)
                mlp_tile(ti, job[2], job[3], job[4], pre.pop(ti))

        def mlp_tile(ti, dst_ap, dst_bufs, Tn, pre):
            i = ti % 2
            xi, xt = pre
            for fg in range(8):
                bk = pb[fg % 3]
                for s4 in range(4):
                    fb = fg * 4 + s4
                    for kc in range(8):
                        T(lambda e, fb=fb, s4=s4, kc=kc, bk=bk: e.matmul(bk[:, s4 * 128:s4 * 128 + Tn], lhsT=Wu[:, kc, fb * 128:(fb + 1) * 128], rhs=xt[:, kc, :Tn],
                                                                         start=kc == 0, stop=kc == 7), [xt, arenaB, WB[0]], [bk])
                r = tf[fg % 2]
                A(lambda e, bk=bk, r=r: e.activation(out=r.ap.rearrange("p (h t) -> p h t", h=8)[:, 0:4, :Tn],
                                                     in_=bk.ap.rearrange("p (h t) -> p h t", h=4)[:, :, :Tn], func=AF.Relu), [bk], [r])
                h2 = tb[fg // 2]
                G(lambda e, r=r, h2=h2, fg=fg: e.tensor_tensor(out=h2.ap.rearrange("p (h t) -> p h t", h=8)[:, (fg % 2) * 4:(fg % 2) * 4 + 4, :Tn],
                                                               in0=r.ap.rearrange("p (h t) -> p h t", h=8)[:, 0:4, :Tn],
                                                               in1=r.ap.rearrange("p (h t) -> p h t", h=8)[:, 0:4, :Tn], op=ALU.mult), [r], [h2])
            for half in range(2):
                for fc in range(32):
                    T(lambda e, half=half, fc=fc: e.matmul(pb[3 + half][:Tn, :], lhsT=tb[fc // 8][:, (fc % 8) * 128:(fc % 8) * 128 + Tn],
                                                           rhs=Wd[:, fc, half * 512:(half + 1) * 512], start=fc == 0, stop=fc == 31),
                      [tb[fc // 8], arenaB, WB[1]], [pb[3 + half]])
            yb = yo[i]
            for half in range(2):
                V(lambda e, half=half: e.tensor_tensor(out=yb[:Tn, half * 512:(half + 1) * 512], in0=xi[:Tn, half * 512:(half + 1) * 512],
                                                       in1=pb[3 + half][:Tn, :], op=ALU.add), [xi, pb[3 + half]], [yb])
            store_rows(dst_ap, yb, Tn, i, dst_bufs)

        if stage >= 2:
            load_mlp_w(0)
            load_norm_w(0, nffn[0:1, :])
            jobs = [(ybuf[ti * 128:(ti + 1) * 128, :], [ytile[ti]], ybuf[ti * 128:(ti + 1) * 128, :], [ytile[ti]], 128) for ti in range(32)]
            jobs.append((ybuf[4096:4128, :], [ytile[32]], ybuf[4096:4128, :], [ytile[32]], 32))
            mlp_pass(jobs)

        if stage >= 3:
            Wg3 = arena[:, 0:24576].rearrange("p (k n) -> p k n", k=8)
            Wao = arena[:, 24576:32768].rearrange("p (k n) -> p k n", k=8)
            Mt = arena[:, 32768:45056]
            MtB = Buf("Mt", Mt)
            fr = arena[:, 45056:65536]

            def carve(name, off, n, dt=BF16):
                a_ = fr[:, off:off + n]
                return Buf(name, a_.bitcast(F32) if dt == F32 else a_)
            vaug = [carve("vaug%d" % j, j * 1040, 1040) for j in range(3)]
            KT = [carve("KT%d" % j, 3120 + j * 1024, 1024) for j in range(3)]
            QTe = [carve("QTe%d" % j, 6192 + j * 1024, 1024) for j in range(2)]
            QTo = [carve("QTo%d" % j, 8240 + j * 1024, 1024) for j in range(2)]
            accsts = [carve("accst%d" % j, 10288 + j * 2080, 2080, F32) for j in range(2)]
            accst = accsts[0]
            qnrep = carve("qnrep", 14448, 128, F32)
            knrep = carve("knrep", 14576, 128, F32)
            onesc = carve("onesc", 14704, 32, F32)
            flagc = carve("flagc", 14736, 32, F32)
            accl = [carve("accl%d" % j, j * 2080, 2080, F32) for j in range(3)]
            p3bufs = vaug + KT + QTe + QTo + accsts + accl + [qnrep, knrep, onesc, flagc, MtB]
            V(lambda e: e.memset(onesc[:], 1.0), w=[onesc, arenaB] + p3bufs)
            P.dma("sync", lambda e: e.dma_start(out=qnrep[:], in_=qn_w[0:1, :].partition_broadcast(128)), g_c3, writes=[qnrep])
            P.dma("sync", lambda e: e.dma_start(out=knrep[:], in_=kn_w[0:1, :].partition_broadcast(128)), g_c3, writes=[knrep])
            V(lambda e: e.tensor_copy(out=flagc[:], in_=flag[:, 0:1].to_broadcast([128, 16])), [flag], [flagc])
            for j_ in range(2):
                V(lambda e, j_=j_: e.memset(QTe[j_][:], 0.0), w=[QTe[j_]])
                V(lambda e, j_=j_: e.memset(QTo[j_][:], 0.0), w=[QTo[j_]])
            load_norm_w(1, nmix[1:2, :])
            allY = ytile[:32]

            def rows_ap(base, start, stride):
                if stride == 1:
                    return base[start:start + 128]
                if len(base.shape) == 3:
                    return base.rearrange("(m r) a d -> r m a d", r=stride)[start % stride, start // stride:start // stride + 128]
                return base.rearrange("(m r) d -> r m d", r=stride)[start % stride, start // stride:start // stride + 128]

            def qk_norm(p0, p1, rep, dst, Tn=128, dst_bf=None, want_f32=True):
                sq = tf[0]
                for half, pp in enumerate((p0, p1)):
                    A(lambda e, half=half, pp=pp: e.activation(out=sq[:Tn, half * 512:(half + 1) * 512], in_=pp[:Tn, :], func=AF.Square), [pp], [sq])
                sm = small[0]
                V(lambda e: e.tensor_reduce(out=sm[:Tn, :], in_=sq.ap.rearrange("p (h d) -> p h d", h=16)[:Tn], axis=AX.X, op=ALU.add), [sq], [sm])
                A(lambda e: e.activation(out=sm[:Tn, :], in_=sm[:Tn, :], func=AF.Ln, scale=1.0 / 64, bias=epsb[:Tn, 0:1]), [sm, epsb], [sm])
                A(lambda e: e.activation(out=sm[:Tn, :], in_=sm[:Tn, :], func=AF.Exp, scale=-0.5), [sm], [sm])
                tq = sq
                for half, pp in enumerate((p0, p1)):
                    V(lambda e, half=half, pp=pp: e.tensor_tensor(out=tq.ap.rearrange("p (h d) -> p h d", h=16)[:Tn, half * 8:(half + 1) * 8, :],
                                                                  in0=pp.ap.rearrange("p (h d) -> p h d", h=8)[:Tn],
                                                                  in1=sm[:Tn, half * 8:(half + 1) * 8].unsqueeze(2).to_broadcast([Tn, 8, 64]), op=ALU.mult),
                      [pp, sm], [tq])
                t3 = tq.ap.rearrange("p (h d) -> p h d", h=16)[:Tn]
                r3 = rep[:Tn, 0:64].unsqueeze(1).to_broadcast([Tn, 16, 64])
                if dst_bf is not None:
                    V(lambda e: e.tensor_tensor(out=dst_bf.ap.rearrange("p (h d) -> p h d", h=16)[:Tn], in0=t3, in1=r3, op=ALU.mult), [tq, rep], [dst_bf])
                if want_f32:
                    G(lambda e: e.tensor_tensor(out=dst.ap.rearrange("p (h d) -> p h d", h=16)[:Tn], in0=t3, in1=r3, op=ALU.mult), [tq, rep], [dst])

            def sample_qkv(g):
                ti = tcount[0]
                tcount[0] += 1
                i = ti % 2
                xi, xt = norm_T(ybuf[4096:4128, :], 32, 1, [ytile[32]], i)
                for part, b0 in ((1, 0), (2, 2)):
                    for half in range(2):
                        for kc in range(8):
                            T(lambda e, part=part, b0=b0, half=half, kc=kc: e.matmul(pb[b0 + half][:32, :], lhsT=xt[:, kc, :32],
                                                                                     rhs=Wg3[:, kc, part * 1024 + half * 512:part * 1024 + (half + 1) * 512],
                                                                                     start=kc == 0, stop=kc == 7), [xt, arenaB, WB[part]], [pb[b0 + half]])
                kf, vf = tf[1], tf[2]
                qk_norm(pb[0], pb[1], knrep, kf, 32)
                for half in range(2):
                    A(lambda e, half=half: e.copy(out=vf[:32, half * 512:(half + 1) * 512], in_=pb[2 + half][:32, :]), [pb[2 + half]], [vf])
                for dst_, bufs in ((kvs[g], []), (skvd[g], [skvB[g]])):
                    P.dma("sync", lambda e, dst_=dst_: e.dma_start(out=dst_[:, 0, :], in_=kf[:32, :]), g_kv[0], reads=[kf], writes=bufs)
                    P.dma("sync", lambda e, dst_=dst_: e.dma_start(out=dst_[:, 1, :], in_=vf[:32, :]), g_kv[1], reads=[vf], writes=bufs)
                for half in range(2):
                    for kc in range(8):
                        T(lambda e, half=half, kc=kc: e.matmul(pb[half][:32, :], lhsT=xt[:, kc, :32], rhs=Wg3[:, kc, half * 512:(half + 1) * 512],
                                                               start=kc == 0, stop=kc == 7), [xt, arenaB, WB[0]], [pb[half]])
                qf_ = tf[3]
                qk_norm(pb[0], pb[1], qnrep, qf_, 32)
                P.dma("sync", lambda e: e.dma_start(out=sqd[g][:, :], in_=qf_[:32, :]), g_kv[0], reads=[qf_], writes=[skvB[g]])

            tcount = [0]

            def attn_A(g, start, stride, own, j, kv_rows, qs, pre):
                xi, xt = pre
                for part, b0 in ((1, 0), (2, 2)):
                    for half in range(2):
                        for kc in range(8):
                            T(lambda e, part=part, b0=b0, half=half, kc=kc: e.matmul(pb[b0 + half][:, :], lhsT=xt[:, kc, :],
                                                                                     rhs=Wg3[:, kc, part * 1024 + half * 512:part * 1024 + (half + 1) * 512],
                                                                                     start=kc == 0, stop=kc == 7), [xt, arenaB, WB[part]], [pb[b0 + half]])
                kf, vf = tf[1], tf[2]
                kbf = tb[0]
                keep = kv_rows is not None
                qk_norm(pb[0], pb[1], knrep, kf, dst_bf=kbf, want_f32=keep)
                va = vaug[j]
                va3 = va.ap.rearrange("p (h d) -> p h d", h=16)
                for half in range(2):
                    A(lambda e, half=half: e.copy(out=va3[:, half * 8:(half + 1) * 8, 0:64], in_=pb[2 + half].ap.rearrange("p (h d) -> p h d", h=8)),
                      [pb[2 + half]], [va])
                    if keep:
                        A(lambda e, half=half: e.copy(out=vf[:, half * 512:(half + 1) * 512], in_=pb[2 + half][:, :]), [pb[2 + half]], [vf])
                cst = onesc if own else flagc
                G(lambda e: e.tensor_copy(out=va3[:, :, 64:65], in_=cst[:, :].unsqueeze(2)), [cst], [va])
                yield
                if own:
                    for half in range(2):
                        for kc in range(8):
                            T(lambda e, half=half, kc=kc: e.matmul(pb[half][:, :], lhsT=xt[:, kc, :], rhs=Wg3[:, kc, half * 512:(half + 1) * 512],
                                                                   start=kc == 0, stop=kc == 7), [xt, arenaB, WB[0]], [pb[half]])
                if keep:
                    P.dma("sync", lambda e: e.dma_start(out=rows_ap(kvp[g], kv_rows[0], kv_rows[1])[:, 0, :], in_=kf[:, :]), g_kv[0], reads=[kf])
                    P.dma("sync", lambda e: e.dma_start(out=rows_ap(kvp[g], kv_rows[0], kv_rows[1])[:, 1, :], in_=vf[:, :]), g_kv[1], reads=[vf])
                yield
                for c in range(8):
                    T(lambda e, c=c: e.transpose(out=pT[:, c * 128:(c + 1) * 128], in_=kbf[:, c * 128:(c + 1) * 128], identity=ident[:, :]), [kbf, ident], [pT])
                A(lambda e: e.copy(out=KT[j][:, :], in_=pT[:, :]), [pT], [KT[j]])
                if not own:
                    return
                qf_ = tf[3]
                qbf = tb[1]
                qk_norm(pb[0], pb[1], qnrep, qf_, dst_bf=qbf, want_f32=False)
                yield
                for c in range(8):
                    T(lambda e, c=c: e.transpose(out=pT[:, c * 128:(c + 1) * 128], in_=qbf[:, c * 128:(c + 1) * 128], identity=ident[:, :]), [qbf, ident], [pT])
                A(lambda e: e.copy(out=QTe[qs][0:64, :], in_=pT[0:64, :]), [pT], [QTe[qs]])
                V(lambda e: e.tensor_copy(out=QTo[qs][64:128, :], in_=pT[64:128, :]), [pT], [QTo[qs]])

            def attn_B(g, start, stride, j, jprev, qs):
                ast = accsts[qs]
                pv = pb[6]
                def scores(c):
                    ps = pb[4 + c % 2]
                    for hl, QTx in enumerate((QTe[qs], QTo[qs])):
                        for jj, Kx in enumerate((KT[jprev], KT[j])):
                            T(lambda e, hl=hl, QTx=QTx, jj=jj, Kx=Kx, ps=ps, c=c: e.matmul(ps[:, hl * 256 + jj * 128:hl * 256 + jj * 128 + 128],
                                                                                          lhsT=Kx[:, c * 128:(c + 1) * 128], rhs=QTx[:, c * 128:(c + 1) * 128],
                                                                                          start=True, stop=True), [Kx, QTx], [ps])
                scores(0)
                for c in range(8):
                    ps = pb[4 + c % 2]
                    if c + 1 < 8:
                        scores(c + 1)
                    E = tb[2 + c % 2]
                    A(lambda e, ps=ps, E=E: e.activation(out=E[:, 0:512], in_=ps[:, :], func=AF.Exp, scale=0.125), [ps], [E])
                    Pm = tb[4 + c % 2]
                    mo = (g * 16 + 2 * c) * 256
                    V(lambda e, E=E, Pm=Pm, mo=mo: e.tensor_tensor(out=Pm[:, 0:512], in0=E[:, 0:512], in1=Mt[:, mo:mo + 512], op=ALU.mult), [E, MtB], [Pm])
                    for hl in range(2):
                        h = 2 * c + hl
                        o_ = pv[:, (h % 7) * 65:(h % 7) * 65 + 65]
                        T(lambda e, hl=hl, h=h, o_=o_, Pm=Pm: e.matmul(o_, lhsT=Pm[:, hl * 256:hl * 256 + 128], rhs=vaug[jprev][:, h * 65:(h + 1) * 65],
                                                                       start=True, stop=False), [Pm, vaug[jprev]], [pv])
                        T(lambda e, hl=hl, h=h, o_=o_, Pm=Pm: e.matmul(o_, lhsT=Pm[:, hl * 256 + 128:hl * 256 + 256], rhs=vaug[j][:, h * 65:(h + 1) * 65],
                                                                       start=False, stop=True), [Pm, vaug[j]], [pv])
                        if h in (6, 13, 15):
                            r0 = (h // 7) * 455
                            n_ = (h % 7 + 1) * 65
                            if h == 13:
                                V(lambda e, r0=r0, n_=n_: e.tensor_copy(out=ast[:, r0:r0 + n_], in_=pv[:, 0:n_]), [pv], [ast])
                            else:
                                A(lambda e, r0=r0, n_=n_: e.copy(out=ast[:, r0:r0 + n_], in_=pv[:, 0:n_]), [pv], [ast])
                    yield
                P.dma("sync", lambda e: e.dma_start(out=rows_ap(accd[g], start - 2048, stride), in_=ast[:, :]), g_ast[qs], reads=[ast], writes=[accb[g]])

            def interleave(ga, gb):
                def step(g_, n):
                    if g_ is None:
                        return True
                    for _ in range(n):
                        try:
                            next(g_)
                        except StopIteration:
                            return True
                    return False
                done_a = step(ga, 1)
                step(gb, 100)
                if not done_a:
                    step(ga, 100)

            pendB = None
            kslot = 0
            qcount = 0
            for g, (w_, dil) in enumerate(GROUPS):
                interleave(None, pendB)
                pendB = None
                for part in (1, 2, 0):
                    load_w(Wg3[:, :, part * 1024:(part + 1) * 1024], wqkv, 8, 1024, col0=part * 3072 + g * 1024, wi=part, extra=p3bufs if g == 0 else ())
                if g == 0:
                    for c0 in range(0, 12288, 2048):
                        P.dma("gpsimd", lambda e, c0=c0: e.dma_start(out=Mt[:, c0:c0 + 2048], in_=mtab_d[:, c0:c0 + 2048]), g_w, writes=[MtB], after=[arenaB] + p3bufs)
                    load_w(Wao, awo, 8, 1024, wi=4, extra=p3bufs)
                nown = 16 // dil
                keep0 = 4096 - w_
                sample_qkv(g)
                tiles = []
                for r in range(dil):
                    tiles.append((2048 - w_ + r, False, None))
                    for jj in range(nown):
                        start = 2048 + jj * 128 * dil + r
                        tiles.append((start, True, (start - keep0, dil) if start >= keep0 else None))
                tis = []
                for _ in tiles:
                    tis.append(tcount[0])
                    tcount[0] += 1
                load_x(rows_ap(ybuf, tiles[0][0], dil), 128, allY, tis[0] % 2)
                load_x(rows_ap(ybuf, tiles[1][0], dil), 128, allY, tis[1] % 2)
                pre = {0: norm_T(None, 128, 1, allY, tis[0] % 2, preloaded=True)}
                for k_, (start, own, kvr) in enumerate(tiles):
                    if k_ + 1 < len(tiles):
                        pre[k_ + 1] = norm_T(None, 128, 1, allY, tis[k_ + 1] % 2, preloaded=True)
                    if k_ + 2 < len(tiles):
                        load_x(rows_ap(ybuf, tiles[k_ + 2][0], dil), 128, allY, tis[k_ + 2] % 2)
                    jprev = kslot % 3
                    kslot += 1
                    j = kslot % 3
                    qs = qcount % 2
                    ga = attn_A(g, start, dil, own, j, kvr, qs, pre.pop(k_))
                    interleave(ga, pendB)
                    pendB = None
                    if own:
                        pendB = attn_B(g, start, dil, j, jprev, qs)
                        qcount += 1
            interleave(None, pendB)

        if stage >= 4:
            W0 = arena[:, 0:24576]
            M0 = arena[:, 32768:45056]

            def cv(name, reg, off, n, dt=BF16):
                a_ = reg[:, off:off + n]
                return Buf(name, a_.bitcast(F32) if dt == F32 else a_)
            ct = [cv("ct%d" % j, W0, j * 4096, 4096, F32) for j in range(2)]
            vnew = [cv("vnew%d" % g, W0, 8192 + g * 1040, 1040) for g in range(3)]
            sqb = [cv("sqb%d" % g, W0, 11312 + g * 1024, 1024) for g in range(3)]
            skb = [cv("skb%d" % g, W0, 14384 + g * 1024, 1024) for g in range(3)]
            KTn = cv("KTn", W0, 17456, 256)
            QTne = cv("QTne", W0, 17712, 256)
            QTno = cv("QTno", W0, 17968, 256)
            vbf = cv("vbf", W0, 18224, 1024)
            pvaug = cv("pvaug", W0, 19248, 1040)
            stabT = cv("stabT", W0, 20288, 3328, F32)
            scs = [cv("sc_%d" % j, W0, 23616 + j * 32, 32, F32) for j in range(2)]
            pps = [cv("pp_%d" % j, W0, 23680 + j * 32, 32, F32) for j in range(2)]
            pvaug2 = cv("pvaug2", M0, 10752, 1040)
            sntab = cv("sntab", M0, 0, 3072, F32)
            selq = cv("selq", M0, 3072, 4096)
            selk = cv("selk", M0, 7168, 1024)
            Pn = cv("Pn", M0, 8192, 512)
            En = cv("En", M0, 8704, 1024, F32)
            sbufs = ct + vnew + sqb + skb + [KTn, QTne, QTno, vbf, pvaug, pvaug2, stabT, sntab, selq, selk, Pn, En] + scs + pps
            p3bufs = p3bufs + sbufs
            aft0 = [arenaB, MtB, WB[0], WB[1], WB[2]]
            P.dma("sync", lambda e: e.dma_start(out=stabT[:, :], in_=stab_d[:, :]), g_c3, writes=sbufs, after=aft0)
            P.dma("sync", lambda e: e.dma_start(out=sntab[:32, :], in_=sntab_d[:, :]), g_c3, writes=[sntab], after=aft0)
            for c0 in range(0, 4096, 2048):
                P.dma("gpsimd", lambda e, c0=c0: e.dma_start(out=selq[:32, c0:c0 + 2048], in_=selq_d[:, c0:c0 + 2048]), g_w, writes=[selq], after=aft0)
            P.dma("gpsimd", lambda e: e.dma_start(out=selk[:, :], in_=selk_d[:, :]), g_w, writes=[selk], after=aft0)
            V(lambda e: e.memset(QTne[:], 0.0), w=[QTne])
            V(lambda e: e.memset(QTno[:], 0.0), w=[QTno])
            for g in range(3):
                P.dma("gpsimd", lambda e, g=g: e.dma_start(out=sqb[g][:32, :], in_=sqd[g][:, :]), g_w, reads=[skvB[g]], writes=[sqb[g]], after=aft0)
                P.dma("gpsimd", lambda e, g=g: e.dma_start(out=skb[g][:32, :], in_=skvd[g][:, 0, :]), g_w, reads=[skvB[g]], writes=[skb[g]], after=aft0)
                P.dma("gpsimd", lambda e, g=g: e.dma_start(out=vnew[g].ap.rearrange("p (h d) -> p h d", h=16)[:32, :, 0:64],
                                                            in_=skvd[g][:, 1, :].rearrange("p (h d) -> p h d", h=16)), g_w, reads=[skvB[g]], writes=[vnew[g]], after=aft0)
                V(lambda e, g=g: e.memset(vnew[g].ap.rearrange("p (h d) -> p h d", h=16)[:32, :, 64:65], 1.0), w=[vnew[g]])
            accP = [pb[4], pb[5], pb[6]]
            first = [True, True, True]
            units = [(0, 0, 1, list(range(8)))] + [(1, r, 4, [r, r + 4]) for r in range(4)] + [(2, s_, 16, [s_]) for s_ in range(8)]
            g_qb = [P.group("qb%d" % j) for j in range(4)]
            qbr = [tf[2], tf[3], tf[4], tf[5]]

            def p3s_units():
                qlist = []
                for b in range(4):
                    for u, (g, r0, dil, qs) in enumerate(units):
                        for s_ in qs:
                            qlist.append((b, u, g, s_))

                def issue_qb(n):
                    if n < len(qlist):
                        b_, u_, g_, s__ = qlist[n]
                        t_ = 8 * b_ + s__
                        P.dma("sync", lambda e, n=n, g_=g_, t_=t_: e.dma_start(out=qbr[n % 4][:, :], in_=sqd[g_][t_:t_ + 1, :].partition_broadcast(128)),
                              g_qb[n % 4], reads=[skvB[g_]], writes=[qbr[n % 4]])
                for n in range(3):
                    issue_qb(n)
                uc = 0
                qn = 0
                for b in range(4):
                    for u, (g, r0, dil, qs) in enumerate(units):
                        cb = ct[uc % 2]
                        gq = g_cache[uc % 2]
                        uc += 1
                        P.dma("sync", lambda e, g=g, b=b, r0=r0, dil=dil, cb=cb: e.dma_start(out=cb.ap.rearrange("p (a d) -> p a d", a=2),
                                                                                             in_=rows_ap(caches[g][b], r0, dil)), gq, writes=[cb])
                        A(lambda e, cb=cb: e.copy(out=vbf[:, :], in_=cb[:, 1024:2048]), [cb], [vbf])
                        for s_ in qs:
                            t = 8 * b + s_
                            par = qn % 2
                            qb = qbr[qn % 4]
                            issue_qb(qn + 3)
                            qn += 1
                            prod = tf[par]
                            sc_, pp_ = scs[par], pps[par]
                            pva = pvaug if par == 0 else pvaug2
                            V(lambda e, cb=cb, prod=prod, qb=qb: e.tensor_tensor(out=prod[:, :], in0=cb[:, 0:1024], in1=qb[:, :], op=ALU.mult), [cb, qb], [prod])
                            V(lambda e, prod=prod, sc_=sc_: e.tensor_reduce(out=sc_[:, 0:16], in_=prod.ap.rearrange("p (h d) -> p h d", h=16), axis=AX.X, op=ALU.add), [prod], [sc_])
                            A(lambda e, sc_=sc_: e.activation(out=sc_[:, 0:16], in_=sc_[:, 0:16], func=AF.Exp, scale=0.125), [sc_], [sc_])
                            so = (u * 8 + s_) * 16
                            V(lambda e, so=so, sc_=sc_, pp_=pp_: e.tensor_tensor(out=pp_[:, 0:16], in0=sc_[:, 0:16], in1=stabT[:, so:so + 16], op=ALU.mult), [sc_, stabT], [pp_])
                            pv3 = pva.ap.rearrange("p (h d) -> p h d", h=16)
                            G(lambda e, pv3=pv3, pp_=pp_: e.tensor_tensor(out=pv3[:, :, 0:64], in0=vbf.ap.rearrange("p (h d) -> p h d", h=16),
                                                                          in1=pp_[:, 0:16].unsqueeze(2).to_broadcast([128, 16, 64]), op=ALU.mult), [vbf, pp_], [pva])
                            G(lambda e, pv3=pv3, pp_=pp_: e.tensor_copy(out=pv3[:, :, 64:65], in_=pp_[:, 0:16].unsqueeze(2)), [pp_], [pva])
                            for k3, (c0, cw) in enumerate(((0, 455), (455, 455), (910, 130))):
                                T(lambda e, t=t, k3=k3, c0=c0, cw=cw, st_=first[k3], pva=pva: e.matmul(accP[k3][:32, 0:cw], lhsT=selk[:, t * 32:(t + 1) * 32], rhs=pva[:, c0:c0 + cw],
                                                                                                       start=st_, stop=False, skip_group_check=True), [selk, pva], [accP[k3]])
                                first[k3] = False
                            yield

            def p3s_tail():
                for g in range(3):
                    for src_, dsts in ((skb[g], [(KTn, 0, 128)]), (sqb[g], [(QTne, 0, 64), (QTno, 64, 128)])):
                        for c in range(8):
                            T(lambda e, c=c, src_=src_: e.transpose(out=pT[:, c * 32:(c + 1) * 32], in_=src_[:32, c * 128:(c + 1) * 128], identity=ident[:32, :32]),
                              [src_, ident], [pT])
                        for (d_, p0, p1) in dsts:
                            A(lambda e, d_=d_, p0=p0, p1=p1: e.copy(out=d_[p0:p1, :], in_=pT[p0:p1, 0:256]), [pT], [d_])
                    ps = pb[0]
                    for h in range(16):
                        QTx = QTne if h % 2 == 0 else QTno
                        c = h // 2
                        T(lambda e, h=h, c=c, QTx=QTx: e.matmul(ps[:32, h * 32:(h + 1) * 32], lhsT=KTn[:, c * 32:(c + 1) * 32], rhs=QTx[:, c * 32:(c + 1) * 32],
                                                                start=True, stop=True), [KTn, QTx], [ps])
                    A(lambda e: e.activation(out=En[:32, 0:512], in_=ps[:32, :], func=AF.Exp, scale=0.125), [ps], [En])
                    V(lambda e, g=g: e.tensor_tensor(out=Pn[:32, 0:512], in0=En[:32, 0:512], in1=sntab[:32, g * 512:(g + 1) * 512], op=ALU.mult), [En, sntab], [Pn])
                    for h in range(16):
                        k3 = h // 7
                        T(lambda e, h=h, k3=k3, g=g: e.matmul(accP[k3][:32, (h % 7) * 65:(h % 7) * 65 + 65], lhsT=Pn[:32, h * 32:(h + 1) * 32], rhs=vnew[g][:32, h * 65:(h + 1) * 65],
                                                              start=False, stop=(g == 2), skip_group_check=True), [Pn, vnew[g]], [accP[k3]])
                A(lambda e: e.copy(out=accst[:32, 0:455], in_=accP[0][:32, 0:455]), [accP[0]], [accst])
                V(lambda e: e.tensor_copy(out=accst[:32, 455:910], in_=accP[1][:32, 0:455]), [accP[1]], [accst])
                A(lambda e: e.copy(out=accst[:32, 910:1040], in_=accP[2][:32, 0:130]), [accP[2]], [accst])


        if stage >= 4:
            def final_tile(t, Tn, rows, ybufs, sample):
                i = t % 2
                xi = xin[i]
                P.dma("sync", lambda e: e.dma_start(out=xi[:Tn, :], in_=rows), g_in[i], reads=ybufs, writes=[xi])
                if not sample:
                    for gg in range(3):
                        P.dma("sync", lambda e, gg=gg: e.dma_start(out=accl[gg][:, :], in_=accd[gg][t * 128:(t + 1) * 128, :]), g_acc[gg], reads=[accb[gg]], writes=[accl[gg]] + vaug + KT + QTe + QTo)
                    V(lambda e: e.tensor_tensor(out=accl[0][:, :], in0=accl[0][:, :], in1=accl[1][:, :], op=ALU.add), [accl[0], accl[1]], [accl[0]])
                    V(lambda e: e.tensor_tensor(out=accl[0][:, :], in0=accl[0][:, :], in1=accl[2][:, :], op=ALU.add), [accl[0], accl[2]], [accl[0]])
                    ac = accl[0]
                else:
                    ac = accst
                yield
                a3 = ac.ap.rearrange("p (h d) -> p h d", h=16)[:Tn]
                sm = small[1]
                V(lambda e: e.reciprocal(out=sm[:Tn, :].unsqueeze(2), in_=a3[:, :, 64:65]), [ac], [sm])
                onb = tb[0]
                V(lambda e: e.tensor_tensor(out=onb.ap.rearrange("p (h d) -> p h d", h=16)[:Tn], in0=a3[:, :, 0:64],
                                            in1=sm[:Tn, :].unsqueeze(2).to_broadcast([Tn, 16, 64]), op=ALU.mult), [ac, sm], [onb])
                for c in range(8):
                    T(lambda e, c=c: e.transpose(out=pT[:, c * 128:c * 128 + Tn], in_=onb[:Tn, c * 128:(c + 1) * 128], identity=ident[:Tn, :Tn]), [onb, ident], [pT])
                yield
                oT_ = tb[1]
                A(lambda e: e.copy(out=oT_.ap.rearrange("p (c t) -> p c t", c=8)[:, :, :Tn], in_=pT.ap.rearrange("p (c t) -> p c t", c=8)[:, :, :Tn]), [pT], [oT_])
                yield
                for half in range(2):
                    for c in range(8):
                        T(lambda e, half=half, c=c: e.matmul(pb[half][:Tn, :], lhsT=oT_[:, c * 128:c * 128 + Tn], rhs=Wao[:, c, half * 512:(half + 1) * 512],
                                                             start=c == 0, stop=c == 7), [oT_, arenaB, WB[4]], [pb[half]])
                yield
                yb = yo[i]
                for half in range(2):
                    V(lambda e, half=half: e.tensor_tensor(out=yb[:Tn, half * 512:(half + 1) * 512], in0=xi[:Tn, half * 512:(half + 1) * 512],
                                                           in1=pb[half][:Tn, :], op=ALU.add), [xi, pb[half]], [yb])
                store_rows(rows, yb, Tn, i, ybufs)

            gu = p3s_units()
            for t in range(16):
                gf = final_tile(t, 128, ybuf[2048 + t * 128:2048 + (t + 1) * 128, :], [ytile[16 + t]], False)
                nq = 0
                for _ in gf:
                    next(gu, None)
                    nq += 1
                while nq < 6:
                    next(gu, None)
                    nq += 1
            for _ in gu:
                pass
            p3s_tail()
            for _ in final_tile(0, 32, ybuf[4096:4128, :], [ytile[32]], True):
                pass
            allov2 = p3bufs
            for kc in range(8):
                for c0 in range(0, 4096, 2048):
                    P.dma("gpsimd", lambda e, kc=kc, c0=c0: e.dma_start(out=Wu[:, kc, c0:c0 + 2048], in_=wup[1, kc * 128:(kc + 1) * 128, c0:c0 + 2048]),
                          GW[0], writes=[WB[0]], after=[arenaB] + allov2)
            for kc in range(32):
                P.dma("gpsimd", lambda e, kc=kc: e.dma_start(out=Wd[:, kc, :], in_=wdn[1, kc * 128:(kc + 1) * 128, :]), GW[1], writes=[WB[1]], after=[arenaB] + allov2)
            load_norm_w(0, nffn[1:2, :])
            jobs = [(ybuf[2048 + t * 128:2048 + (t + 1) * 128, :], [ytile[16 + t]], yp[t * 128:(t + 1) * 128, :], [], 128) for t in range(16)]
            jobs.append((ybuf[4096:4128, :], [ytile[32]], ys[:, :], [], 32))
            mlp_pass(jobs)
        P.emit()
    return nc


def _consts():
    c = {}
    c["idn"] = np.eye(128, dtype=np.float32)
    s = np.arange(128)
    c["hmask"] = ((s[:, None] // 64 == s[None, :] // 64) & (s[:, None] <= s[None, :])).astype(np.float32)
    s8 = np.arange(32)
    c["hmask8"] = ((s8[:, None] // 8 == s8[None, :] // 8) & (s8[:, None] <= s8[None, :])).astype(np.float32)
    t = np.arange(1024) % 128
    c["rst"] = np.stack([(t % 64 != 0), (t % 8 != 0)]).astype(np.float32)
    n = 48
    slopes = (2.0 ** (-8.0 * np.arange(1, n + 1) / n)).reshape(3, 16)
    k = np.arange(128)[:, None]
    q = np.arange(128)[None, :]
    mt = np.zeros((128, 3, 16, 2, 128), np.float32)
    for g, (w, d) in enumerate(GROUPS):
        for h in range(16):
            cgh = slopes[g, h] * d
            dist_prev = q - k + 128
            mt[:, g, h, 0, :] = np.where(k >= q, np.exp(-cgh * dist_prev), 0.0)
            dist_cur = q - k
            mt[:, g, h, 1, :] = np.where(k <= q, np.exp(-cgh * np.maximum(dist_cur, 0)), 0.0)
    c["mtab"] = mt.reshape(128, 48 * 256)
    bm = np.zeros((32, 4), np.float32)
    for b in range(4):
        bm[8 * b:8 * b + 8, b] = 1.0
    c["bmask"] = bm
    stab = np.zeros((128, 13, 8, 16), np.float64)
    i_ = np.arange(128)[:, None]
    for s_ in range(8):
        stab[:, 0, s_, :] = np.where(i_ >= s_, np.exp(-slopes[0][None, :] * (128 + s_ - i_)), 0.0)
    for r in range(4):
        for m, s_ in enumerate((r, r + 4)):
            j = 128 + m - i_
            stab[:, 1 + r, s_, :] = np.where(i_ >= m, np.exp(-slopes[1][None, :] * 4.0 * j), 0.0)
    for s_ in range(8):
        stab[:, 5 + s_, s_, :] = np.exp(-slopes[2][None, :] * 16.0 * (128 - i_))
    c["stab"] = stab.reshape(128, 13 * 8 * 16).astype(np.float32)
    sn = np.zeros((32, 3, 16, 32), np.float64)
    for g, (w, d) in enumerate(GROUPS):
        for kb in range(4):
            for n_ in range(8):
                for s_ in range(8):
                    dd = s_ - n_
                    if dd >= 0 and dd % d == 0:
                        sn[kb * 8 + n_, g, :, kb * 8 + s_] = np.exp(-slopes[g] * dd)
    c["sntab"] = sn.reshape(32, 3 * 16 * 32).astype(np.float32)
    sq_ = np.zeros((32, 32, 128), np.float32)
    sk_ = np.zeros((128, 32, 32), np.float32)
    for t in range(32):
        sq_[t, t, :] = 1.0
        sk_[:, t, t] = 1.0
    c["selq"] = sq_.reshape(32, 32 * 128)
    c["selk"] = sk_.reshape(128, 32 * 32)
    return c


_NC_CACHE = {}


def kernel(x_prompt, x_sample, state_hgrn, cache_kv_w128, cache_kv_w512, cache_kv_w2048,
           hg_lb_logits, hg_w_q, hg_w_f, hg_w_i, hg_w_g, hg_w_o, hg_norm_o,
           att_w_qkv, att_w_o, att_q_norm, att_k_norm,
           norm_mix, norm_ffn, ffn_w_up, ffn_w_down, _stage=99, _debug=False):
    f = lambda a: np.ascontiguousarray(np.asarray(a, dtype=np.float32))
    x_prompt, x_sample, state_hgrn = f(x_prompt), f(x_sample), f(state_hgrn)
    cs = [f(cache_kv_w128), f(cache_kv_w512), f(cache_kv_w2048)]
    shared = {
        "lbl": f(hg_lb_logits), "hg_q": f(hg_w_q)[0], "hg_f": f(hg_w_f)[0], "hg_i": f(hg_w_i)[0], "hg_g": f(hg_w_g)[0],
        "hg_o": f(hg_w_o)[0], "gno": f(hg_norm_o), "wqkv": f(att_w_qkv)[0], "awo": f(att_w_o)[0], "qn_w": f(att_q_norm),
        "kn_w": f(att_k_norm), "nmix": f(norm_mix), "nffn": f(norm_ffn), "wup": f(ffn_w_up), "wdn": f(ffn_w_down),
    }
    shared.update(_consts())
    in_maps = []
    for c in range(8):
        b, half = c // 2, c % 2
        m = dict(shared)
        if half == 0:
            xs = np.concatenate([np.zeros((2048, 1024), np.float32), x_prompt[b, :2048]], 0)
        else:
            xs = x_prompt[b]
        m["xs"] = np.ascontiguousarray(xs)
        m["xsm"] = np.ascontiguousarray(x_sample[4 * c:4 * c + 4].reshape(32, 1024))
        m["st_in"] = np.ascontiguousarray(state_hgrn[0, 4 * c:4 * c + 4])
        for g in range(3):
            m["c%d" % g] = np.ascontiguousarray(cs[g][0, 4 * c:4 * c + 4].reshape(4, GROUPS[g][0], 2, 1024))
        m["flag"] = np.full((128, 1), float(half), np.float32)
        in_maps.append(m)
    if _stage not in _NC_CACHE:
        _NC_CACHE[_stage] = build_nc(_stage)
    nc = _NC_CACHE[_stage]
    res = run_bass_kernel_spmd(nc, in_maps, core_ids=list(range(8)))
    R = res.results
    if _debug:
        return R
    y_prompt = np.zeros((4, 4096, 1024), np.float32)
    y_sample = np.zeros((32, 8, 1024), np.float32)
    st_p = np.zeros((1, 4, 8, 128, 128), np.float32)
    st_s = np.zeros((1, 32, 8, 128, 128), np.float32)
    kvp = [np.zeros((1, 4, GROUPS[g][0], 2, 16, 64), np.float32) for g in range(3)]
    kvs = [np.zeros((1, 32, 8, 2, 16, 64), np.float32) for g in range(3)]
    for c in range(8):
        b, half = c // 2, c % 2
        y_prompt[b, half * 2048:(half + 1) * 2048] = R[c]["yp"]
        y_sample[4 * c:4 * c + 4] = R[c]["ys"].reshape(4, 8, 1024)
        st_s[0, 4 * c:4 * c + 4] = R[c]["st_s"]
        for g in range(3):
            kvs[g][0, 4 * c:4 * c + 4] = R[c]["kvs%d" % g].reshape(4, 8, 2, 16, 64)
        if half == 1:
            st_p[0, b] = R[c]["st_p"]
            for g in range(3):
                kvp[g][0, b] = R[c]["kvp%d" % g].reshape(GROUPS[g][0], 2, 16, 64)
    return (y_prompt, y_sample, st_p, st_s, kvp[0], kvs[0], kvp[1], kvs[1], kvp[2], kvs[2])
```

```python
import numpy as np
from contextlib import ExitStack
import concourse.bass as bass
import concourse.mybir as mybir
from concourse.bass_utils import run_bass_kernel_spmd

F32 = mybir.dt.float32
BF16 = mybir.dt.bfloat16
AF = mybir.ActivationFunctionType
ALU = mybir.AluOpType
AX = mybir.AxisListType

GROUPS = ((128, 1), (512, 4), (2048, 16))
EPS = 1e-6


class Buf:
    __slots__ = ("name", "ap", "last_w", "readers")

    def __init__(self, name, ap):
        self.name = name
        self.ap = ap
        self.last_w = None
        self.readers = []

    def __getitem__(self, idx):
        return self.ap[idx]


class DmaGroup:
    def __init__(self, name):
        self.name = name
        self.cnt = 0


class Prog:
    ENGS = ("sync", "scalar", "vector", "gpsimd", "tensor")

    def __init__(self, nc, stack):
        self.nc = nc
        self.stack = stack
        self.q = {e: [] for e in self.ENGS}
        self.seq = {e: 0 for e in self.ENGS}
        self.waited = {e: {} for e in self.ENGS}
        self.sems = {}
        self.groups = []
        self.targets = {e: set() for e in self.ENGS}
        self.gmap = {}
        for e in self.ENGS:
            self.sems["p_" + e] = stack.enter_context(nc.semaphore("p_" + e))

    def sbuf(self, name, shape, dtype):
        return self.stack.enter_context(self.nc.sbuf_tensor("s_" + name, list(shape), dtype)).ap()

    def psum(self, name, shape, dtype=F32):
        return self.stack.enter_context(self.nc.psum_tensor("ps_" + name, list(shape), dtype)).ap()

    def group(self, name):
        g = DmaGroup(name)
        self.sems["d_" + name] = self.stack.enter_context(self.nc.semaphore("d_" + name))
        self.groups.append(g)
        self.gmap["d_" + name] = g
        return g

    def _deps(self, eng, reads, writes):
        need = {}

        def add(tok):
            if tok is not None and need.get(tok[0], 0) < tok[1]:
                need[tok[0]] = tok[1]
        for b in reads:
            add(b.last_w)
        for b in writes:
            add(b.last_w)
            for r in b.readers:
                add(r)
        waits = []
        w = self.waited[eng]
        for k, v in need.items():
            if k == "p_tensor" and eng == "tensor":
                continue
            if k.startswith("d_"):
                v = self.gmap[k].cnt
            if w.get(k, 0) < v:
                w[k] = v
                waits.append((k, v))
                if k.startswith("p_"):
                    self.targets[k[2:]].add(v)
        return waits

    def _commit(self, tok, reads, writes):
        for b in reads:
            b.readers.append(tok)
            if len(b.readers) > 64:
                m = {}
                for k, v in b.readers:
                    if m.get(k, 0) < v:
                        m[k] = v
                b.readers = list(m.items())
        for b in writes:
            b.last_w = tok
            b.readers = []

    def op(self, eng, fn, reads=(), writes=()):
        waits = self._deps(eng, reads, writes)
        self.seq[eng] += 1
        tok = ("p_" + eng, self.seq[eng])
        self.q[eng].append((fn, waits, "p_" + eng, self.seq[eng]))
        self._commit(tok, reads, writes)
        return tok

    def dma(self, eng, fn, grp, reads=(), writes=(), after=()):
        waits = self._deps(eng, reads, list(writes) + list(after))
        grp.cnt += 16
        tok = ("d_" + grp.name, grp.cnt)
        self.q[eng].append((fn, waits, "d_" + grp.name, None))
        self._commit(tok, reads, writes)
        return tok

    def emit(self):
        toks = [("d_" + g.name, g.cnt) for g in self.groups if g.cnt]
        for e in self.ENGS:
            if self.seq[e]:
                toks.append(("p_" + e, self.seq[e]))
                self.targets[e].add(self.seq[e])
        self.q["sync"].append((None, toks, None, None))
        rank = {}
        for e in self.ENGS:
            rank[e] = {idx: r + 1 for r, idx in enumerate(sorted(self.targets[e]))}
        with self.nc.Block() as block:
            def run(engname):
                def body(e):
                    for fn, waits, inc, own in self.q[engname]:
                        for k, v in waits:
                            if k.startswith("p_"):
                                v = rank[k[2:]][v]
                            e.wait_ge(self.sems[k], v)
                        if fn is not None:
                            ins = fn(e)
                            if own is None:
                                ins.then_inc(self.sems[inc], 16)
                            elif own in rank[engname]:
                                ins.then_inc(self.sems[inc], 1)
                return body
            block.sync(run("sync"))
            block.scalar(run("scalar"))
            block.vector(run("vector"))
            block.gpsimd(run("gpsimd"))
            block.tensor(run("tensor"))


def build_nc(stage=99):
    nc = bass.Bass("TRN2", target_bir_lowering=False)

    def din(name, shape):
        return nc.dram_tensor(name, list(shape), F32, kind="ExternalInput").ap()

    def dout(name, shape):
        return nc.dram_tensor(name, list(shape), F32, kind="ExternalOutput").ap()

    xs = din("xs", [4096, 1024])
    xsm = din("xsm", [32, 1024])
    st_in = din("st_in", [4, 8, 128, 128])
    caches = [din("c%d" % g, [4, GROUPS[g][0], 2, 1024]) for g in range(3)]
    lbl = din("lbl", [3, 1024])
    hw = {k: din("hg_" + k, [1024, 1024]) for k in ("q", "f", "i", "g", "o")}
    gno = din("gno", [1, 1024])
    wqkv = din("wqkv", [1024, 9216])
    awo = din("awo", [1024, 1024])
    qn_w = din("qn_w", [1, 64])
    kn_w = din("kn_w", [1, 64])
    nmix = din("nmix", [2, 1024])
    nffn = din("nffn", [2, 1024])
    wup = din("wup", [2, 1024, 4096])
    wdn = din("wdn", [2, 4096, 1024])
    idn = din("idn", [128, 128])
    hmask_d = din("hmask", [128, 128])
    hmask8_d = din("hmask8", [32, 32])
    rst_d = din("rst", [2, 1024])
    flag_d = din("flag", [128, 1])
    mtab_d = din("mtab", [128, 48 * 256])
    bmask_d = din("bmask", [32, 4])
    stab_d = din("stab", [128, 13 * 8 * 16])
    sntab_d = din("sntab", [32, 3 * 16 * 32])
    selq_d = din("selq", [32, 32 * 128])
    selk_d = din("selk", [128, 32 * 32])

    yp = dout("yp", [2048, 1024])
    ys = dout("ys", [32, 1024])
    st_p = dout("st_p", [8, 128, 128])
    st_s = dout("st_s", [4, 8, 128, 128])
    kvp = [dout("kvp%d" % g, [GROUPS[g][0], 2, 1024]) for g in range(3)]
    kvs = [dout("kvs%d" % g, [32, 2, 1024]) for g in range(3)]

    ybuf = nc.dram_tensor("ybuf", [4096 + 32, 1024], F32, kind="Internal").ap()
    accd = [nc.dram_tensor("accd%d" % g, [2048, 1040], F32, kind="Internal").ap() for g in range(3)]
    skvd = [nc.dram_tensor("skvd%d" % g, [32, 2, 1024], F32, kind="Internal").ap() for g in range(3)]
    sqd = [nc.dram_tensor("sqd%d" % g, [32, 1024], F32, kind="Internal").ap() for g in range(3)]
    skvB = [Buf("skvB%d" % g, skvd[g]) for g in range(3)]

    with ExitStack() as st:
        P = Prog(nc, st)
        V = lambda fn, r=(), w=(): P.op("vector", fn, r, w)
        A = lambda fn, r=(), w=(): P.op("scalar", fn, r, w)
        G = lambda fn, r=(), w=(): P.op("gpsimd", fn, r, w)
        T = lambda fn, r=(), w=(): P.op("tensor", fn, r, w)

        g_const = P.group("const")
        g_wrep = P.group("wrep")
        g_c3 = P.group("c3")
        g_w = P.group("w")
        g_in = [P.group("in%d" % i) for i in range(3)]
        g_out = [P.group("out%d" % i) for i in range(3)]
        g_misc = P.group("misc")
        g_acc = [P.group("acc%d" % i) for i in range(3)]
        g_ast = [P.group("ast%d" % i) for i in range(2)]
        g_kv = [P.group("kv%d" % i) for i in range(2)]
        g_cache = [P.group("cache%d" % i) for i in range(2)]

        ytile = [Buf("ybuf%d" % t, ybuf) for t in range(33)]
        accb = [Buf("accd%d" % g, accd[g]) for g in range(3)]

        arena = P.sbuf("arena", [128, 65536], BF16)
        arenaB = Buf("arena", arena)
        WB = [Buf("WB%d" % i, arena) for i in range(5)]
        GW = [P.group("gw%d" % i) for i in range(5)]

        def f32s(name, shape):
            return Buf(name, P.sbuf(name, shape, F32))

        def b16s(name, shape):
            return Buf(name, P.sbuf(name, shape, BF16))

        ident = b16s("ident", [128, 128])
        ones_bf = b16s("ones_bf", [128, 128])
        epsb = f32s("epsb", [128, 1])
        flag = f32s("flag", [128, 1])
        hmask = b16s("hmask", [128, 128])
        hmask8 = b16s("hmask8", [32, 32])
        wrep = [f32s("wrep0", [128, 1024])] * 2
        xin = [f32s("xin%d" % i, [128, 1024]) for i in range(3)]
        yo = xin
        xn = [b16s("xn%d" % i, [128, 1024]) for i in range(2)]
        xnT = [b16s("xnT%d" % i, [128, 8, 128]) for i in range(2)]
        ssb = [f32s("ssb%d" % i, [128, 1]) for i in range(2)]
        tf = [f32s("tf%d" % i, [128, 1024]) for i in range(6)]
        tb = [b16s("tb%d" % i, [128, 1024]) for i in range(8)]
        small = [f32s("small%d" % i, [128, 16]) for i in range(4)]
        tbx = [b16s("tbx%d" % i, [128, 1024]) for i in range(4)]
        gate2 = f32s("gate2", [128, 1024])
        decs = [f32s("dec%d" % i, [128, 32]) for i in range(2)]
        rstdB = f32s("rstdB", [128, 128])

        pb = [Buf("pb%d" % i, P.psum("pb%d" % i, [128, 512], F32)) for i in range(7)]
        pT = Buf("pT", P.psum("pT", [128, 1024], BF16))

        P.dma("gpsimd", lambda e: e.dma_start(out=ident[:], in_=idn[:, :]), g_w, writes=[ident])
        P.dma("gpsimd", lambda e: e.dma_start(out=hmask[:], in_=hmask_d[:, :]), g_w, writes=[hmask])
        P.dma("gpsimd", lambda e: e.dma_start(out=hmask8[:], in_=hmask8_d[:, :]), g_w, writes=[hmask8])
        P.dma("sync", lambda e: e.dma_start(out=flag[:], in_=flag_d[:, :]), g_const, writes=[flag])
        V(lambda e: e.memset(epsb[:], EPS), w=[epsb])
        V(lambda e: e.memset(ones_bf[:], 1.0), w=[ones_bf])

        def load_w(dst_ap3, src2d, nk, ncols, col0=0, wi=0, extra=()):
            for kc in range(nk):
                for c0 in range(0, ncols, 2048):
                    cw = min(2048, ncols - c0)
                    P.dma("gpsimd", lambda e, kc=kc, c0=c0, cw=cw: e.dma_start(
                        out=dst_ap3[:, kc, c0:c0 + cw],
                        in_=src2d[kc * 128:(kc + 1) * 128, col0 + c0:col0 + c0 + cw]),
                        GW[wi], writes=[WB[wi]], after=[arenaB] + list(extra))

        def load_norm_w(slot, src_row):
            P.dma("sync", lambda e: e.dma_start(out=wrep[slot][:], in_=src_row.partition_broadcast(128)),
                  g_wrep, writes=[wrep[slot]])

        cnt = {"in": 0, "out": 0}

        def load_x(src_ap, Tn, src_bufs, i):
            xi = xin[i]
            P.dma("sync", lambda e: e.dma_start(out=xi[:Tn, :], in_=src_ap), g_in[i], reads=src_bufs, writes=[xi])

        def norm_T(src_ap, Tn, wslot, src_bufs, i, preloaded=False, xs=None):
            xs = i if xs is None else xs
            xi, xnb, xt, sb = xin[xs], xn[i], xnT[i], ssb[i]
            if not preloaded:
                load_x(src_ap, Tn, src_bufs, xs)
            jk = xnb
            A(lambda e: e.activation(out=jk[:Tn, :], in_=xi[:Tn, :], func=AF.Square, accum_out=sb[:Tn, :]),
              [xi], [jk, sb])
            A(lambda e: e.activation(out=sb[:Tn, :], in_=sb[:Tn, :], func=AF.Ln, scale=1.0 / 1024, bias=epsb[:Tn, 0:1]),
              [sb, epsb], [sb])
            A(lambda e: e.activation(out=sb[:Tn, :], in_=sb[:Tn, :], func=AF.Exp, scale=-0.5), [sb], [sb])
            V(lambda e: e.scalar_tensor_tensor(out=xnb[:Tn, :], in0=xi[:Tn, :], scalar=sb[:Tn, 0:1], in1=wrep[wslot][:Tn, :],
                                               op0=ALU.mult, op1=ALU.mult), [xi, sb, wrep[wslot]], [xnb])
            for c in range(8):
                T(lambda e, c=c: e.transpose(out=pT[:, c * 128:c * 128 + Tn], in_=xnb[:Tn, c * 128:(c + 1) * 128],
                                             identity=ident[:Tn, :Tn]), [xnb, ident], [pT])
            A(lambda e: e.copy(out=xt[:, :, :Tn], in_=pT.ap.rearrange("p (c t) -> p c t", c=8)[:, :, :Tn]), [pT], [xt])
            return xi, xt

        def store_rows(dst_ap, src_buf, Tn, i, dst_bufs):
            P.dma("sync", lambda e: e.dma_start(out=dst_ap, in_=src_buf[:Tn, :]), g_out[i], reads=[src_buf], writes=dst_bufs)

        def v3(buf, Tn):
            return buf.ap.rearrange("p (h t) -> p h t", h=8)[:, :, :Tn]

        Wq = arena[:, 0:8192].rearrange("p (k n) -> p k n", k=8)
        Wf = arena[:, 8192:16384].rearrange("p (k n) -> p k n", k=8)
        Wi = arena[:, 16384:24576].rearrange("p (k n) -> p k n", k=8)
        Wg = arena[:, 24576:32768].rearrange("p (k n) -> p k n", k=8)
        Wo = arena[:, 32768:40960].rearrange("p (k n) -> p k n", k=8)
        for wi_, (k_, W_) in enumerate((("q", Wq), ("f", Wf), ("g", Wg), ("i", Wi), ("o", Wo))):
            load_w(W_, hw[k_], 8, 1024, wi=wi_)
        load_norm_w(0, nmix[0:1, :])
        ov = arena[:, 40960:65536].bitcast(F32)
        ovb = [Buf("ov%d" % i, ov[:, i * 1024:(i + 1) * 1024]) for i in range(12)]
        lbT, omlT, wgT, rstT, rstT8, Sst, gate, oT = ovb[0:8]
        lb3 = ovb[8]
        lbv = lbl.rearrange("l (h k) -> k l h", h=8)
        lraw = Buf("lraw", lb3.ap[:, 0:24].rearrange("p (l h) -> p l h", l=3))
        P.dma("sync", lambda e: e.dma_start(out=lraw[:], in_=lbv, allow_slow_non_contiguous=True), g_const, writes=[lb3])
        P.dma("sync", lambda e: e.dma_start(out=lb3[:, 48:56], in_=gno.rearrange("o (h v) -> v (o h)", h=8), allow_slow_non_contiguous=True), g_const, writes=[lb3])
        P.dma("sync", lambda e: e.dma_start(out=rstT[:], in_=rst_d[0:1, :].partition_broadcast(128)), g_const, writes=[rstT])
        P.dma("sync", lambda e: e.dma_start(out=rstT8[:], in_=rst_d[1:2, :].partition_broadcast(128)), g_const, writes=[rstT8])
        bmask = f32s("bmask", [32, 4])
        P.dma("sync", lambda e: e.dma_start(out=bmask[:], in_=bmask_d[:, :]), g_const, writes=[bmask])
        A(lambda e: e.activation(out=lb3[:, 0:24], in_=lb3[:, 0:24], func=AF.Exp), [lb3], [lb3])
        V(lambda e: e.tensor_tensor(out=lb3[:, 24:32], in0=lb3[:, 0:8], in1=lb3[:, 8:16], op=ALU.add), [lb3], [lb3])
        V(lambda e: e.tensor_tensor(out=lb3[:, 24:32], in0=lb3[:, 24:32], in1=lb3[:, 16:24], op=ALU.add), [lb3], [lb3])
        V(lambda e: e.reciprocal(out=lb3[:, 24:32], in_=lb3[:, 24:32]), [lb3], [lb3])
        V(lambda e: e.tensor_tensor(out=lb3[:, 32:40], in0=lb3[:, 0:8], in1=lb3[:, 24:32], op=ALU.mult), [lb3], [lb3])
        V(lambda e: e.tensor_scalar(out=lb3[:, 40:48], in0=lb3[:, 32:40], scalar1=-1.0, scalar2=1.0, op0=ALU.mult, op1=ALU.add),
          [lb3], [lb3])
        V(lambda e: e.tensor_copy(out=v3(lbT, 128), in_=lb3[:, 32:40].unsqueeze(2).to_broadcast([128, 8, 128])), [lb3], [lbT])
        V(lambda e: e.tensor_copy(out=v3(omlT, 128), in_=lb3[:, 40:48].unsqueeze(2).to_broadcast([128, 8, 128])), [lb3], [omlT])
        V(lambda e: e.tensor_copy(out=v3(wgT, 128), in_=lb3[:, 48:56].unsqueeze(2).to_broadcast([128, 8, 128])), [lb3], [wgT])
        V(lambda e: e.memset(Sst[:], 0.0), w=[Sst])
        SbfA, SbfB = tbx[2], tbx[3]
        V(lambda e: e.memset(SbfA[:], 0.0), w=[SbfA])

        def hgrn_A(ti, src_ap, Tn, sample, pre=None):
            i = ti % 2
            C = 8 if sample else 64
            nch = Tn // C
            xi, xt = pre if pre is not None else norm_T(src_ap, Tn, 0, [], i, xs=ti % 3)
            sg, qf, ff, cum, Aex, kE = tf
            qp, kp, kppT, vb = tb[0 + i], tb[2 + i], tb[4 + i], tb[6 + i]
            kpp = tbx[0]
            gate = (ovb[6], gate2)[i]
            dec = decs[i]
            rs = rstT8 if sample else rstT
            yield
            for h in range(8):
                for kc in range(8):
                    T(lambda e, h=h, kc=kc: e.matmul(pb[h // 4][:, (h % 4) * 128:(h % 4) * 128 + Tn], lhsT=Wq[:, kc, h * 128:(h + 1) * 128],
                                                     rhs=xt[:, kc, :Tn], start=kc == 0, stop=kc == 7), [xt, arenaB, WB[0]], [pb[h // 4]])
            for h in range(8):
                for kc in range(8):
                    T(lambda e, h=h, kc=kc: e.matmul(pb[2 + h // 4][:, (h % 4) * 128:(h % 4) * 128 + Tn], lhsT=Wf[:, kc, h * 128:(h + 1) * 128],
                                                     rhs=xt[:, kc, :Tn], start=kc == 0, stop=kc == 7), [xt, arenaB, WB[1]], [pb[2 + h // 4]])

            yield

            def p3(j, Tn=Tn):
                return pb[j].ap.rearrange("p (h t) -> p h t", h=4)[:, :, :Tn]

            def s3(buf, half, Tn=Tn):
                return buf.ap.rearrange("p (h t) -> p h t", h=8)[:, half * 4:(half + 1) * 4, :Tn]
            for half in range(2):
                A(lambda e, half=half: e.activation(out=s3(sg, half), in_=p3(half), func=AF.Sigmoid), [pb[half]], [sg])
                V(lambda e, half=half: e.tensor_tensor(out=s3(qf, half), in0=p3(half), in1=s3(sg, half), op=ALU.mult), [pb[half], sg], [qf])
            for half in range(2):
                A(lambda e, half=half: e.activation(out=s3(ff, half), in_=p3(2 + half), func=AF.Sigmoid), [pb[2 + half]], [ff])
            V(lambda e: e.tensor_tensor(out=v3(ff, Tn), in0=v3(ff, Tn), in1=v3(omlT, Tn), op=ALU.mult), [ff, omlT], [ff])
            V(lambda e: e.tensor_tensor(out=v3(ff, Tn), in0=v3(ff, Tn), in1=v3(lbT, Tn), op=ALU.add), [ff, lbT], [ff])
            yield
            A(lambda e: e.activation(out=v3(cum, Tn), in_=v3(ff, Tn), func=AF.Ln), [ff], [cum])
            if not sample:
                V(lambda e: e.tensor_tensor_scan(out=cum[:, :], data0=rs[:, :], data1=cum[:, :], initial=0.0, op0=ALU.mult, op1=ALU.add),
                  [cum, rs], [cum])
            else:
                for h in range(8):
                    V(lambda e, h=h: e.tensor_tensor_scan(out=cum[:, h * 128:h * 128 + Tn], data0=rs[:, h * 128:h * 128 + Tn],
                                                          data1=cum[:, h * 128:h * 128 + Tn], initial=0.0, op0=ALU.mult, op1=ALU.add),
                      [cum, rs], [cum])
            yield
            A(lambda e: e.activation(out=v3(Aex, Tn), in_=v3(cum, Tn), func=AF.Exp), [cum], [Aex])
            A(lambda e: e.activation(out=v3(kE, Tn), in_=v3(cum, Tn), func=AF.Exp, scale=-1.0), [cum], [kE])
            yield
            V(lambda e: e.tensor_scalar(out=v3(ff, Tn), in0=v3(ff, Tn), scalar1=-1.0, scalar2=1.0, op0=ALU.mult, op1=ALU.add), [ff], [ff])
            G(lambda e: e.tensor_tensor(out=v3(qp, Tn), in0=v3(qf, Tn), in1=v3(Aex, Tn), op=ALU.mult), [qf, Aex], [qp])
            V(lambda e: e.tensor_tensor(out=v3(kE, Tn), in0=v3(kE, Tn), in1=v3(ff, Tn), op=ALU.mult), [kE, ff], [kE])
            A(lambda e: e.copy(out=v3(kp, Tn), in_=v3(kE, Tn)), [kE], [kp])
            a4 = Aex.ap.rearrange("p (h t) -> p h t", h=8)[:, :, :Tn].rearrange("p h (n c) -> p h n c", c=C)
            V(lambda e: e.tensor_tensor(out=kpp.ap.rearrange("p (h t) -> p h t", h=8)[:, :, :Tn].rearrange("p h (n c) -> p h n c", c=C),
                                        in0=kE.ap.rearrange("p (h t) -> p h t", h=8)[:, :, :Tn].rearrange("p h (n c) -> p h n c", c=C),
                                        in1=a4[:, :, :, C - 1:C].to_broadcast([128, 8, nch, C]), op=ALU.mult), [kE, Aex], [kpp])
            V(lambda e: e.tensor_copy(out=dec.ap.rearrange("p (h n) -> p h n", h=8)[:, :, :nch], in_=a4[:, :, :, C - 1]), [Aex], [dec])
            yield
            for h in range(8):
                for kc in range(8):
                    T(lambda e, h=h, kc=kc: e.matmul(pb[h // 4][:, (h % 4) * 128:(h % 4) * 128 + Tn], lhsT=Wg[:, kc, h * 128:(h + 1) * 128],
                                                     rhs=xt[:, kc, :Tn], start=kc == 0, stop=kc == 7), [xt, arenaB, WB[2]], [pb[h // 4]])
            for half in range(2):
                for kc in range(8):
                    T(lambda e, half=half, kc=kc: e.matmul(pb[2 + half][:Tn, :], lhsT=xt[:, kc, :Tn], rhs=Wi[:, kc, half * 512:(half + 1) * 512],
                                                           start=kc == 0, stop=kc == 7), [xt, arenaB, WB[3]], [pb[2 + half]])
            for half in range(2):
                A(lambda e, half=half: e.activation(out=s3(sg, half), in_=p3(half), func=AF.Sigmoid), [pb[half]], [sg])
                V(lambda e, half=half: e.tensor_tensor(out=s3(gate, half), in0=p3(half), in1=s3(sg, half), op=ALU.mult), [pb[half], sg], [gate])
                A(lambda e, half=half: e.copy(out=vb[:Tn, half * 512:(half + 1) * 512], in_=pb[2 + half][:Tn, :]), [pb[2 + half]], [vb])
            G(lambda e: e.tensor_tensor(out=v3(gate, Tn), in0=v3(gate, Tn), in1=v3(wgT, Tn), op=ALU.mult), [gate, wgT], [gate])
            yield
            for h in range(8):
                T(lambda e, h=h: e.transpose(out=pT[:Tn, h * 128:(h + 1) * 128], in_=kpp[:, h * 128:h * 128 + Tn], identity=ident[:, :]),
                  [kpp, ident], [pT])
            A(lambda e: e.copy(out=kppT[:Tn, :], in_=pT[:Tn, :]), [pT], [kppT])
        def hgrn_B(ti, Tn, sample):
            i = ti % 2
            xi = xin[ti % 3]
            qp, kp, kppT, vb = tb[0 + i], tb[2 + i], tb[4 + i], tb[6 + i]
            attnm = tbx[1]
            gate = (ovb[6], gate2)[i]
            dec = decs[i]
            hm = hmask8 if sample else hmask

            def s3(buf, half, Tn=Tn):
                return buf.ap.rearrange("p (h t) -> p h t", h=8)[:, half * 4:(half + 1) * 4, :Tn]
            for hg in range(2):
                hs = range(hg * 4, hg * 4 + 4)
                pa, po, pS = pb[4], pb[5], pb[6]
                for h in hs:
                    T(lambda e, h=h: e.matmul(pa[:Tn, (h % 4) * 128:(h % 4) * 128 + Tn], lhsT=kp[:, h * 128:h * 128 + Tn],
                                              rhs=qp[:, h * 128:h * 128 + Tn], start=True, stop=True), [kp, qp], [pa])
                V(lambda e, hg=hg: e.tensor_tensor(out=attnm.ap.rearrange("p (h t) -> p h t", h=8)[:Tn, hg * 4:hg * 4 + 4, :Tn],
                                                   in0=pa.ap.rearrange("p (h t) -> p h t", h=4)[:Tn, :, :Tn],
                                                   in1=hm[:Tn, :Tn].unsqueeze(1).to_broadcast([Tn, 4, Tn]), op=ALU.mult), [pa, hm], [attnm])
                yield
                if not sample:
                    for h in hs:
                        T(lambda e, h=h: e.matmul(pS[:, (h % 4) * 128:(h % 4 + 1) * 128], lhsT=kppT[0:64, h * 128:(h + 1) * 128],
                                                  rhs=vb[0:64, h * 128:(h + 1) * 128], start=True, stop=True), [kppT, vb], [pS])
                    for h in hs:
                        V(lambda e, h=h: e.scalar_tensor_tensor(out=Sst[:, h * 128:(h + 1) * 128], in0=Sst[:, h * 128:(h + 1) * 128],
                                                                scalar=dec[:, h * 4:h * 4 + 1], in1=pS[:, (h % 4) * 128:(h % 4 + 1) * 128],
                                                                op0=ALU.mult, op1=ALU.add), [Sst, dec, pS], [Sst])
                    A(lambda e, hg=hg: e.copy(out=SbfB[:, hg * 512:(hg + 1) * 512], in_=Sst[:, hg * 512:(hg + 1) * 512]), [Sst], [SbfB])
                    for h in hs:
                        o_ = po[:, (h % 4) * 128:(h % 4) * 128 + 128]
                        T(lambda e, h=h, o_=o_: e.matmul(o_, lhsT=vb[:, h * 128:(h + 1) * 128], rhs=attnm[:, h * 128:(h + 1) * 128],
                                                         start=True, stop=False), [vb, attnm], [po])
                        T(lambda e, h=h, o_=o_: e.matmul(o_[:, 0:64], lhsT=SbfA[:, h * 128:(h + 1) * 128], rhs=qp[:, h * 128:h * 128 + 64],
                                                         start=False, stop=False), [SbfA, qp], [po])
                        T(lambda e, h=h, o_=o_: e.matmul(o_[:, 64:128], lhsT=SbfB[:, h * 128:(h + 1) * 128], rhs=qp[:, h * 128 + 64:h * 128 + 128],
                                                         start=False, stop=True), [SbfB, qp], [po])
                    yield
                    for h in hs:
                        T(lambda e, h=h: e.matmul(pS[:, (h % 4) * 128:(h % 4 + 1) * 128], lhsT=kppT[64:128, h * 128:(h + 1) * 128],
                                                  rhs=vb[64:128, h * 128:(h + 1) * 128], start=True, stop=True), [kppT, vb], [pS])
                    for h in hs:
                        V(lambda e, h=h: e.scalar_tensor_tensor(out=Sst[:, h * 128:(h + 1) * 128], in0=Sst[:, h * 128:(h + 1) * 128],
                                                                scalar=dec[:, h * 4 + 1:h * 4 + 2], in1=pS[:, (h % 4) * 128:(h % 4 + 1) * 128],
                                                                op0=ALU.mult, op1=ALU.add), [Sst, dec, pS], [Sst])
                    A(lambda e, hg=hg: e.copy(out=SbfA[:, hg * 512:(hg + 1) * 512], in_=Sst[:, hg * 512:(hg + 1) * 512]), [Sst], [SbfA])
                else:
                    po2 = pb[0]
                    for h in hs:
                        o_ = po[:, (h % 4) * 128:(h % 4) * 128 + 32]
                        T(lambda e, h=h, o_=o_: e.matmul(o_, lhsT=vb[:32, h * 128:(h + 1) * 128], rhs=attnm[:32, h * 128:h * 128 + 32],
                                                         start=True, stop=True), [vb, attnm], [po])
                    for b in range(4):
                        Sbf = SbfA if b % 2 == 0 else SbfB
                        G(lambda e, b=b, Sbf=Sbf, hg=hg: e.tensor_copy(out=Sbf[:, hg * 512:(hg + 1) * 512], in_=sS[b][:, hg * 512:(hg + 1) * 512]),
                          [sS[b]], [Sbf])
                        for h in hs:
                            T(lambda e, h=h, b=b, Sbf=Sbf: e.matmul(po2[:, (h % 4) * 128 + 8 * b:(h % 4) * 128 + 8 * b + 8],
                                                                    lhsT=Sbf[:, h * 128:(h + 1) * 128],
                                                                    rhs=qp[:, h * 128 + 8 * b:h * 128 + 8 * b + 8], start=True, stop=True),
                              [Sbf, qp], [po2])
                    for b in range(4):
                        vm = xn[1 - i]
                        V(lambda e, b=b, vm=vm, hg=hg: e.tensor_scalar(out=vm[:32, hg * 512:(hg + 1) * 512], in0=vb[:32, hg * 512:(hg + 1) * 512],
                                                                       scalar1=bmask[:32, b:b + 1], scalar2=None, op0=ALU.mult), [vb, bmask], [vm])
                        for h in hs:
                            T(lambda e, h=h, b=b, vm=vm: e.matmul(pS[:, (h % 4) * 128:(h % 4 + 1) * 128], lhsT=kppT[0:32, h * 128:(h + 1) * 128],
                                                                  rhs=vm[0:32, h * 128:(h + 1) * 128], start=True, stop=True), [kppT, vm], [pS])
                        for h in hs:
                            V(lambda e, h=h, b=b: e.scalar_tensor_tensor(out=sS[b][:, h * 128:(h + 1) * 128], in0=sS[b][:, h * 128:(h + 1) * 128],
                                                                         scalar=dec[:, h * 4 + b:h * 4 + b + 1],
                                                                         in1=pS[:, (h % 4) * 128:(h % 4 + 1) * 128], op0=ALU.mult, op1=ALU.add),
                              [sS[b], dec, pS], [sS[b]])
                A(lambda e, hg=hg: e.copy(out=s3(oT, hg), in_=po.ap.rearrange("p (h t) -> p h t", h=4)[:, :, :Tn]), [po], [oT])
                if sample:
                    V(lambda e, hg=hg: e.tensor_tensor(out=s3(oT, hg), in0=s3(oT, hg), in1=pb[0].ap.rearrange("p (h t) -> p h t", h=4)[:, :, :Tn],
                                                       op=ALU.add), [oT, pb[0]], [oT])
                yield
            sqo = SbfB
            A(lambda e: e.activation(out=v3(sqo, Tn), in_=v3(oT, Tn), func=AF.Square), [oT], [sqo])
            pn = pb[6]
            for h in range(8):
                T(lambda e, h=h: e.matmul(pn[:, :Tn], lhsT=ones_bf[:, :], rhs=sqo[:, h * 128:h * 128 + Tn], start=h == 0, stop=h == 7),
                  [sqo, ones_bf], [pn])
            rstd = rstdB
            A(lambda e: e.activation(out=rstd[:, :Tn], in_=pn[:, :Tn], func=AF.Ln, scale=1.0 / 1024, bias=epsb[:, 0:1]), [pn, epsb], [rstd])
            A(lambda e: e.activation(out=rstd[:, :Tn], in_=rstd[:, :Tn], func=AF.Exp, scale=-0.5), [rstd], [rstd])
            V(lambda e: e.tensor_tensor(out=v3(oT, Tn), in0=v3(oT, Tn), in1=v3(gate, Tn), op=ALU.mult), [oT, gate], [oT])
            onT = attnm
            V(lambda e: e.tensor_tensor(out=v3(onT, Tn), in0=v3(oT, Tn), in1=rstd[:, :Tn].unsqueeze(1).to_broadcast([128, 8, Tn]), op=ALU.mult),
              [oT, rstd], [onT])
            yield
            for half in range(2):
                for h in range(8):
                    T(lambda e, half=half, h=h: e.matmul(pb[4 + half][:Tn, :], lhsT=onT[:, h * 128:h * 128 + Tn], rhs=Wo[:, h, half * 512:(half + 1) * 512],
                                                         start=h == 0, stop=h == 7), [onT, arenaB, WB[4]], [pb[4 + half]])
            for half in range(2):
                V(lambda e, half=half: e.tensor_tensor(out=xi[:Tn, half * 512:(half + 1) * 512], in0=xi[:Tn, half * 512:(half + 1) * 512],
                                                       in1=pb[4 + half][:Tn, :], op=ALU.add), [xi, pb[4 + half]], [xi])
            dst = ybuf[4096:4128, :] if sample else ybuf[ti * 128:(ti + 1) * 128, :]
            store_rows(dst, xi, Tn, ti % 3, [ytile[ti]])

        def run2(ga, gb, mid_at=None, mid=None):
            da, db = ga is None, gb is None
            n_ = 0
            while not (da and db):
                if mid is not None and n_ == mid_at:
                    mid()
                    mid = None
                n_ += 1
                if not da:
                    try:
                        next(ga)
                    except StopIteration:
                        da = True
                if not db:
                    try:
                        next(gb)
                    except StopIteration:
                        db = True
            if mid is not None:
                mid()

        n_ptiles = 32 if stage >= 1 else 2
        load_x(xs[0:128, :], 128, [], 0)
        load_x(xs[128:256, :], 128, [], 1)
        hpre = {0: norm_T(None, 128, 0, [], 0, preloaded=True, xs=0)}
        run2(hgrn_A(0, None, 128, False, pre=hpre.pop(0)), None)
        hpre[1] = norm_T(None, 128, 0, [], 1, preloaded=True, xs=1)
        for ti in range(n_ptiles):
            if ti + 2 < n_ptiles:
                load_x(xs[(ti + 2) * 128:(ti + 3) * 128, :], 128, [], (ti + 2) % 3)
            ga = hgrn_A(ti + 1, None, 128, False, pre=hpre.pop(ti + 1)) if ti + 1 < n_ptiles else None

            def mid(ti=ti):
                if ti + 2 < n_ptiles:
                    hpre[ti + 2] = norm_T(None, 128, 0, [], (ti + 2) % 2, preloaded=True, xs=(ti + 2) % 3)
            run2(ga, hgrn_B(ti, 128, False), mid_at=4, mid=mid)
        P.dma("sync", lambda e: e.dma_start(out=st_p.rearrange("h k v -> k h v"), in_=Sst.ap.rearrange("p (h v) -> p h v", h=8)),
              g_misc, reads=[Sst])
        sS = [ovb[8 + b] for b in range(4)]
        for b in range(4):
            P.dma("sync", lambda e, b=b: e.dma_start(out=sS[b].ap.rearrange("p (h v) -> p h v", h=8), in_=st_in[b].rearrange("h k v -> k h v")),
                  g_misc, writes=[sS[b]])
        run2(hgrn_A(32, xsm[:, :], 32, True), None)
        run2(None, hgrn_B(32, 32, True))
        for b in range(4):
            P.dma("sync", lambda e, b=b: e.dma_start(out=st_s[b].rearrange("h k v -> k h v"), in_=sS[b].ap.rearrange("p (h v) -> p h v", h=8)),
                  g_misc, reads=[sS[b]])

        Wu = arena[:, 0:32768].rearrange("p (k n) -> p k n", k=8)
        Wd = arena[:, 32768:65536].rearrange("p (k n) -> p k n", k=32)
        allov = list(ovb)

        def load_mlp_w(layer):
            for kc in range(8):
                for c0 in range(0, 4096, 2048):
                    P.dma("gpsimd", lambda e, kc=kc, c0=c0: e.dma_start(out=Wu[:, kc, c0:c0 + 2048], in_=wup[layer, kc * 128:(kc + 1) * 128, c0:c0 + 2048]),
                          GW[0], writes=[WB[0]], after=[arenaB] + allov)
            for kc in range(32):
                P.dma("gpsimd", lambda e, kc=kc: e.dma_start(out=Wd[:, kc, :], in_=wdn[layer, kc * 128:(kc + 1) * 128, :]), GW[1], writes=[WB[1]], after=[arenaB] + allov)

        def mlp_pass(jobs):
            pre = {}
            pre[0] = norm_T(jobs[0][0], jobs[0][4], 0, jobs[0][1], 0)
            for ti, job in enumerate(jobs):
                if ti + 1 < len(jobs):
                    nj = jobs[ti + 1]
                    pre[ti + 1] = norm_T(nj[0], nj[4], 0, nj[1], (ti + 1) % 2)
                mlp_tile(ti, job[2], job[3], job[4], pre.pop(ti))

        def mlp_tile(ti, dst_ap, dst_bufs, Tn, pre):
            i = ti % 2
            xi, xt = pre
            for fg in range(8):
                bk = pb[fg % 3]
                for s4 in range(4):
                    fb = fg * 4 + s4
                    for kc in range(8):
                        T(lambda e, fb=fb, s4=s4, kc=kc, bk=bk: e.matmul(bk[:, s4 * 128:s4 * 128 + Tn], lhsT=Wu[:, kc, fb * 128:(fb + 1) * 128], rhs=xt[:, kc, :Tn],
                                                                         start=kc == 0, stop=kc == 7), [xt, arenaB, WB[0]], [bk])
                r = tf[fg % 2]
                A(lambda e, bk=bk, r=r: e.activation(out=r.ap.rearrange("p (h t) -> p h t", h=8)[:, 0:4, :Tn],
                                                     in_=bk.ap.rearrange("p (h t) -> p h t", h=4)[:, :, :Tn], func=AF.Relu), [bk], [r])
                h2 = tb[fg // 2]
                G(lambda e, r=r, h2=h2, fg=fg: e.tensor_tensor(out=h2.ap.rearrange("p (h t) -> p h t", h=8)[:, (fg % 2) * 4:(fg % 2) * 4 + 4, :Tn],
                                                               in0=r.ap.rearrange("p (h t) -> p h t", h=8)[:, 0:4, :Tn],
                                                               in1=r.ap.rearrange("p (h t) -> p h t", h=8)[:, 0:4, :Tn], op=ALU.mult), [r], [h2])
            for half in range(2):
                for fc in range(32):
                    T(lambda e, half=half, fc=fc: e.matmul(pb[3 + half][:Tn, :], lhsT=tb[fc // 8][:, (fc % 8) * 128:(fc % 8) * 128 + Tn],
                                                           rhs=Wd[:, fc, half * 512:(half + 1) * 512], start=fc == 0, stop=fc == 31),
                      [tb[fc // 8], arenaB, WB[1]], [pb[3 + half]])
            yb = yo[i]
            for half in range(2):
                V(lambda e, half=half: e.tensor_tensor(out=yb[:Tn, half * 512:(half + 1) * 512], in0=xi[:Tn, half * 512:(half + 1) * 512],
                                                       in1=pb[3 + half][:Tn, :], op=ALU.add), [xi, pb[3 + half]], [yb])
            store_rows(dst_ap, yb, Tn, i, dst_bufs)

        if stage >= 2:
            load_mlp_w(0)
            load_norm_w(0, nffn[0:1, :])
            jobs = [(ybuf[ti * 128:(ti + 1) * 128, :], [ytile[ti]], ybuf[ti * 128:(ti + 1) * 128, :], [ytile[ti]], 128) for ti in range(32)]
            jobs.append((ybuf[4096:4128, :], [ytile[32]], ybuf[4096:4128, :], [ytile[32]], 32))
            mlp_pass(jobs)

        if stage >= 3:
            Wg3 = arena[:, 0:24576].rearrange("p (k n) -> p k n", k=8)
            Wao = arena[:, 24576:32768].rearrange("p (k n) -> p k n", k=8)
            Mt = arena[:, 32768:45056]
            MtB = Buf("Mt", Mt)
            fr = arena[:, 45056:65536]

            def carve(name, off, n, dt=BF16):
                a_ = fr[:, off:off + n]
                return Buf(name, a_.bitcast(F32) if dt == F32 else a_)
            vaug = [carve("vaug%d" % j, j * 1040, 1040) for j in range(3)]
            KT = [carve("KT%d" % j, 3120 + j * 1024, 1024) for j in range(3)]
            QTe = [carve("QTe%d" % j, 6192 + j * 1024, 1024) for j in range(2)]
            QTo = [carve("QTo%d" % j, 8240 + j * 1024, 1024) for j in range(2)]
            accsts = [carve("accst%d" % j, 10288 + j * 2080, 2080, F32) for j in range(2)]
            accst = accsts[0]
            qnrep = carve("qnrep", 14448, 128, F32)
            knrep = carve("knrep", 14576, 128, F32)
            onesc = carve("onesc", 14704, 32, F32)
            flagc = carve("flagc", 14736, 32, F32)
            accl = [carve("accl%d" % j, j * 2080, 2080, F32) for j in range(3)]
            p3bufs = vaug + KT + QTe + QTo + accsts + accl + [qnrep, knrep, onesc, flagc, MtB]
            V(lambda e: e.memset(onesc[:], 1.0), w=[onesc, arenaB] + p3bufs)
            P.dma("sync", lambda e: e.dma_start(out=qnrep[:], in_=qn_w[0:1, :].partition_broadcast(128)), g_c3, writes=[qnrep])
            P.dma("sync", lambda e: e.dma_start(out=knrep[:], in_=kn_w[0:1, :].partition_broadcast(128)), g_c3, writes=[knrep])
            V(lambda e: e.tensor_copy(out=flagc[:], in_=flag[:, 0:1].to_broadcast([128, 16])), [flag], [flagc])
            for j_ in range(2):
                V(lambda e, j_=j_: e.memset(QTe[j_][:], 0.0), w=[QTe[j_]])
                V(lambda e, j_=j_: e.memset(QTo[j_][:], 0.0), w=[QTo[j_]])
            load_norm_w(1, nmix[1:2, :])
            allY = ytile[:32]

            def rows_ap(base, start, stride):
                if stride == 1:
                    return base[start:start + 128]
                if len(base.shape) == 3:
                    return base.rearrange("(m r) a d -> r m a d", r=stride)[start % stride, start // stride:start // stride + 128]
                return base.rearrange("(m r) d -> r m d", r=stride)[start % stride, start // stride:start // stride + 128]

            def qk_norm(p0, p1, rep, dst, Tn=128, dst_bf=None, want_f32=True):
                sq = tf[0]
                for half, pp in enumerate((p0, p1)):
                    A(lambda e, half=half, pp=pp: e.activation(out=sq[:Tn, half * 512:(half + 1) * 512], in_=pp[:Tn, :], func=AF.Square), [pp], [sq])
                sm = small[0]
                V(lambda e: e.tensor_reduce(out=sm[:Tn, :], in_=sq.ap.rearrange("p (h d) -> p h d", h=16)[:Tn], axis=AX.X, op=ALU.add), [sq], [sm])
                A(lambda e: e.activation(out=sm[:Tn, :], in_=sm[:Tn, :], func=AF.Ln, scale=1.0 / 64, bias=epsb[:Tn, 0:1]), [sm, epsb], [sm])
                A(lambda e: e.activation(out=sm[:Tn, :], in_=sm[:Tn, :], func=AF.Exp, scale=-0.5), [sm], [sm])
                tq = sq
                for half, pp in enumerate((p0, p1)):
                    V(lambda e, half=half, pp=pp: e.tensor_tensor(out=tq.ap.rearrange("p (h d) -> p h d", h=16)[:Tn, half * 8:(half + 1) * 8, :],
                                                                  in0=pp.ap.rearrange("p (h d) -> p h d", h=8)[:Tn],
                                                                  in1=sm[:Tn, half * 8:(half + 1) * 8].unsqueeze(2).to_broadcast([Tn, 8, 64]), op=ALU.mult),
                      [pp, sm], [tq])
                t3 = tq.ap.rearrange("p (h d) -> p h d", h=16)[:Tn]
                r3 = rep[:Tn, 0:64].unsqueeze(1).to_broadcast([Tn, 16, 64])
                if dst_bf is not None:
                    V(lambda e: e.tensor_tensor(out=dst_bf.ap.rearrange("p (h d) -> p h d", h=16)[:Tn], in0=t3, in1=r3, op=ALU.mult), [tq, rep], [dst_bf])
                if want_f32:
                    G(lambda e: e.tensor_tensor(out=dst.ap.rearrange("p (h d) -> p h d", h=16)[:Tn], in0=t3, in1=r3, op=ALU.mult), [tq, rep], [dst])

            def sample_qkv(g):
                ti = tcount[0]
                tcount[0] += 1
                i = ti % 2
                xi, xt = norm_T(ybuf[4096:4128, :], 32, 1, [ytile[32]], i)
                for part, b0 in ((1, 0), (2, 2)):
                    for half in range(2):
                        for kc in range(8):
                            T(lambda e, part=part, b0=b0, half=half, kc=kc: e.matmul(pb[b0 + half][:32, :], lhsT=xt[:, kc, :32],
                                                                                     rhs=Wg3[:, kc, part * 1024 + half * 512:part * 1024 + (half + 1) * 512],
                                                                                     start=kc == 0, stop=kc == 7), [xt, arenaB, WB[part]], [pb[b0 + half]])
                kf, vf = tf[1], tf[2]
                qk_norm(pb[0], pb[1], knrep, kf, 32)
                for half in range(2):
                    A(lambda e, half=half: e.copy(out=vf[:32, half * 512:(half + 1) * 512], in_=pb[2 + half][:32, :]), [pb[2 + half]], [vf])
                for dst_, bufs in ((kvs[g], []), (skvd[g], [skvB[g]])):
                    P.dma("sync", lambda e, dst_=dst_: e.dma_start(out=dst_[:, 0, :], in_=kf[:32, :]), g_kv[0], reads=[kf], writes=bufs)
                    P.dma("sync", lambda e, dst_=dst_: e.dma_start(out=dst_[:, 1, :], in_=vf[:32, :]), g_kv[1], reads=[vf], writes=bufs)
                for half in range(2):
                    for kc in range(8):
                        T(lambda e, half=half, kc=kc: e.matmul(pb[half][:32, :], lhsT=xt[:, kc, :32], rhs=Wg3[:, kc, half * 512:(half + 1) * 512],
                                                               start=kc == 0, stop=kc == 7), [xt, arenaB, WB[0]], [pb[half]])
                qf_ = tf[3]
                qk_norm(pb[0], pb[1], qnrep, qf_, 32)
                P.dma("sync", lambda e: e.dma_start(out=sqd[g][:, :], in_=qf_[:32, :]), g_kv[0], reads=[qf_], writes=[skvB[g]])

            tcount = [0]

            def attn_A(g, start, stride, own, j, kv_rows, qs, pre):
                xi, xt = pre
                for part, b0 in ((1, 0), (2, 2)):
                    for half in range(2):
                        for kc in range(8):
                            T(lambda e, part=part, b0=b0, half=half, kc=kc: e.matmul(pb[b0 + half][:, :], lhsT=xt[:, kc, :],
                                                                                     rhs=Wg3[:, kc, part * 1024 + half * 512:part * 1024 + (half + 1) * 512],
                                                                                     start=kc == 0, stop=kc == 7), [xt, arenaB, WB[part]], [pb[b0 + half]])
                kf, vf = tf[1], tf[2]
                kbf = tb[0]
                keep = kv_rows is not None
                qk_norm(pb[0], pb[1], knrep, kf, dst_bf=kbf, want_f32=keep)
                va = vaug[j]
                va3 = va.ap.rearrange("p (h d) -> p h d", h=16)
                for half in range(2):
                    A(lambda e, half=half: e.copy(out=va3[:, half * 8:(half + 1) * 8, 0:64], in_=pb[2 + half].ap.rearrange("p (h d) -> p h d", h=8)),
                      [pb[2 + half]], [va])
                    if keep:
                        A(lambda e, half=half: e.copy(out=vf[:, half * 512:(half + 1) * 512], in_=pb[2 + half][:, :]), [pb[2 + half]], [vf])
                cst = onesc if own else flagc
                G(lambda e: e.tensor_copy(out=va3[:, :, 64:65], in_=cst[:, :].unsqueeze(2)), [cst], [va])
                yield
                if own:
                    for half in range(2):
                        for kc in range(8):
                            T(lambda e, half=half, kc=kc: e.matmul(pb[half][:, :], lhsT=xt[:, kc, :], rhs=Wg3[:, kc, half * 512:(half + 1) * 512],
                                                                   start=kc == 0, stop=kc == 7), [xt, arenaB, WB[0]], [pb[half]])
                if keep:
                    P.dma("sync", lambda e: e.dma_start(out=rows_ap(kvp[g], kv_rows[0], kv_rows[1])[:, 0, :], in_=kf[:, :]), g_kv[0], reads=[kf])
                    P.dma("sync", lambda e: e.dma_start(out=rows_ap(kvp[g], kv_rows[0], kv_rows[1])[:, 1, :], in_=vf[:, :]), g_kv[1], reads=[vf])
                yield
                for c in range(8):
                    T(lambda e, c=c: e.transpose(out=pT[:, c * 128:(c + 1) * 128], in_=kbf[:, c * 128:(c + 1) * 128], identity=ident[:, :]), [kbf, ident], [pT])
                A(lambda e: e.copy(out=KT[j][:, :], in_=pT[:, :]), [pT], [KT[j]])
                if not own:
                    return
                qf_ = tf[3]
                qbf = tb[1]
                qk_norm(pb[0], pb[1], qnrep, qf_, dst_bf=qbf, want_f32=False)
                yield
                for c in range(8):
                    T(lambda e, c=c: e.transpose(out=pT[:, c * 128:(c + 1) * 128], in_=qbf[:, c * 128:(c + 1) * 128], identity=ident[:, :]), [qbf, ident], [pT])
                A(lambda e: e.copy(out=QTe[qs][0:64, :], in_=pT[0:64, :]), [pT], [QTe[qs]])
                V(lambda e: e.tensor_copy(out=QTo[qs][64:128, :], in_=pT[64:128, :]), [pT], [QTo[qs]])

            def attn_B(g, start, stride, j, jprev, qs):
                ast = accsts[qs]
                pv = pb[6]
                def scores(c):
                    ps = pb[4 + c % 2]
                    for hl, QTx in enumerate((QTe[qs], QTo[qs])):
                        for jj, Kx in enumerate((KT[jprev], KT[j])):
                            T(lambda e, hl=hl, QTx=QTx, jj=jj, Kx=Kx, ps=ps, c=c: e.matmul(ps[:, hl * 256 + jj * 128:hl * 256 + jj * 128 + 128],
                                                                                          lhsT=Kx[:, c * 128:(c + 1) * 128], rhs=QTx[:, c * 128:(c + 1) * 128],
                                                                                          start=True, stop=True), [Kx, QTx], [ps])
                scores(0)
                for c in range(8):
                    ps = pb[4 + c % 2]
                    if c + 1 < 8:
                        scores(c + 1)
                    E = tb[2 + c % 2]
                    A(lambda e, ps=ps, E=E: e.activation(out=E[:, 0:512], in_=ps[:, :], func=AF.Exp, scale=0.125), [ps], [E])
                    Pm = tb[4 + c % 2]
                    mo = (g * 16 + 2 * c) * 256
                    V(lambda e, E=E, Pm=Pm, mo=mo: e.tensor_tensor(out=Pm[:, 0:512], in0=E[:, 0:512], in1=Mt[:, mo:mo + 512], op=ALU.mult), [E, MtB], [Pm])
                    for hl in range(2):
                        h = 2 * c + hl
                        o_ = pv[:, (h % 7) * 65:(h % 7) * 65 + 65]
                        T(lambda e, hl=hl, h=h, o_=o_, Pm=Pm: e.matmul(o_, lhsT=Pm[:, hl * 256:hl * 256 + 128], rhs=vaug[jprev][:, h * 65:(h + 1) * 65],
                                                                       start=True, stop=False), [Pm, vaug[jprev]], [pv])
                        T(lambda e, hl=hl, h=h, o_=o_, Pm=Pm: e.matmul(o_, lhsT=Pm[:, hl * 256 + 128:hl * 256 + 256], rhs=vaug[j][:, h * 65:(h + 1) * 65],
                                                                       start=False, stop=True), [Pm, vaug[j]], [pv])
                        if h in (6, 13, 15):
                            r0 = (h // 7) * 455
                            n_ = (h % 7 + 1) * 65
                            if h == 13:
                                V(lambda e, r0=r0, n_=n_: e.tensor_copy(out=ast[:, r0:r0 + n_], in_=pv[:, 0:n_]), [pv], [ast])
                            else:
                                A(lambda e, r0=r0, n_=n_: e.copy(out=ast[:, r0:r0 + n_], in_=pv[:, 0:n_]), [pv], [ast])
                    yield
                P.dma("sync", lambda e: e.dma_start(out=rows_ap(accd[g], start - 2048, stride), in_=ast[:, :]), g_ast[qs], reads=[ast], writes=[accb[g]])

            def interleave(ga, gb):
                def step(g_, n):
                    if g_ is None:
                        return True
                    for _ in range(n):
                        try:
                            next(g_)
                        except StopIteration:
                            return True
                    return False
                done_a = step(ga, 1)
                done_b = gb is None
                while not (done_a and done_b):
                    if not done_b:
                        done_b = step(gb, 3)
                    if not done_a:
                        done_a = step(ga, 1)

            pendB = None
            kslot = 0
            qcount = 0
            for g, (w_, dil) in enumerate(GROUPS):
                interleave(None, pendB)
                pendB = None
                for part in (1, 2, 0):
                    load_w(Wg3[:, :, part * 1024:(part + 1) * 1024], wqkv, 8, 1024, col0=part * 3072 + g * 1024, wi=part, extra=p3bufs if g == 0 else ())
                if g == 0:
                    for c0 in range(0, 12288, 2048):
                        P.dma("gpsimd", lambda e, c0=c0: e.dma_start(out=Mt[:, c0:c0 + 2048], in_=mtab_d[:, c0:c0 + 2048]), g_w, writes=[MtB], after=[arenaB] + p3bufs)
                    load_w(Wao, awo, 8, 1024, wi=4, extra=p3bufs)
                nown = 16 // dil
                keep0 = 4096 - w_
                sample_qkv(g)
                tiles = []
                for r in range(dil):
                    tiles.append((2048 - w_ + r, False, None))
                    for jj in range(nown):
                        start = 2048 + jj * 128 * dil + r
                        tiles.append((start, True, (start - keep0, dil) if start >= keep0 else None))
                tis = []
                for _ in tiles:
                    tis.append(tcount[0])
                    tcount[0] += 1
                load_x(rows_ap(ybuf, tiles[0][0], dil), 128, allY, tis[0] % 2)
                load_x(rows_ap(ybuf, tiles[1][0], dil), 128, allY, tis[1] % 2)
                pre = {0: norm_T(None, 128, 1, allY, tis[0] % 2, preloaded=True)}
                for k_, (start, own, kvr) in enumerate(tiles):
                    if k_ + 1 < len(tiles):
                        pre[k_ + 1] = norm_T(None, 128, 1, allY, tis[k_ + 1] % 2, preloaded=True)
                    if k_ + 2 < len(tiles):
                        load_x(rows_ap(ybuf, tiles[k_ + 2][0], dil), 128, allY, tis[k_ + 2] % 2)
                    jprev = kslot % 3
                    kslot += 1
                    j = kslot % 3
                    qs = qcount % 2
                    ga = attn_A(g, start, dil, own, j, kvr, qs, pre.pop(k_))
                    interleave(ga, pendB)
                    pendB = None
                    if own:
                        pendB = attn_B(g, start, dil, j, jprev, qs)
                        qcount += 1
            interleave(None, pendB)

        if stage >= 4:
            W0 = arena[:, 0:24576]
            M0 = arena[:, 32768:45056]

            def cv(name, reg, off, n, dt=BF16):
                a_ = reg[:, off:off + n]
                return Buf(name, a_.bitcast(F32) if dt == F32 else a_)
            ct = [cv("ct%d" % j, W0, j * 4096, 4096, F32) for j in range(2)]
            vnew = [cv("vnew%d" % g, W0, 8192 + g * 1040, 1040) for g in range(3)]
            sqb = [cv("sqb%d" % g, W0, 11312 + g * 1024, 1024) for g in range(3)]
            skb = [cv("skb%d" % g, W0, 14384 + g * 1024, 1024) for g in range(3)]
            KTn = cv("KTn", W0, 17456, 256)
            QTne = cv("QTne", W0, 17712, 256)
            QTno = cv("QTno", W0, 17968, 256)
            vbf = cv("vbf", W0, 18224, 1024)
            pvaug = cv("pvaug", W0, 19248, 1040)
            stabT = cv("stabT", W0, 20288, 3328, F32)
            scs = [cv("sc_%d" % j, W0, 23616 + j * 32, 32, F32) for j in range(2)]
            pps = [cv("pp_%d" % j, W0, 23680 + j * 32, 32, F32) for j in range(2)]
            pvaug2 = cv("pvaug2", M0, 10752, 1040)
            sntab = cv("sntab", M0, 0, 3072, F32)
            selq = cv("selq", M0, 3072, 4096)
            selk = cv("selk", M0, 7168, 1024)
            Pn = cv("Pn", M0, 8192, 512)
            En = cv("En", M0, 8704, 1024, F32)
            sbufs = ct + vnew + sqb + skb + [KTn, QTne, QTno, vbf, pvaug, pvaug2, stabT, sntab, selq, selk, Pn, En] + scs + pps
            p3bufs = p3bufs + sbufs
            aft0 = [arenaB, MtB, WB[0], WB[1], WB[2]]
            P.dma("sync", lambda e: e.dma_start(out=stabT[:, :], in_=stab_d[:, :]), g_c3, writes=sbufs, after=aft0)
            P.dma("sync", lambda e: e.dma_start(out=sntab[:32, :], in_=sntab_d[:, :]), g_c3, writes=[sntab], after=aft0)
            for c0 in range(0, 4096, 2048):
                P.dma("gpsimd", lambda e, c0=c0: e.dma_start(out=selq[:32, c0:c0 + 2048], in_=selq_d[:, c0:c0 + 2048]), g_w, writes=[selq], after=aft0)
            P.dma("gpsimd", lambda e: e.dma_start(out=selk[:, :], in_=selk_d[:, :]), g_w, writes=[selk], after=aft0)
            V(lambda e: e.memset(QTne[:], 0.0), w=[QTne])
            V(lambda e: e.memset(QTno[:], 0.0), w=[QTno])
            for g in range(3):
                P.dma("gpsimd", lambda e, g=g: e.dma_start(out=sqb[g][:32, :], in_=sqd[g][:, :]), g_w, reads=[skvB[g]], writes=[sqb[g]], after=aft0)
                P.dma("gpsimd", lambda e, g=g: e.dma_start(out=skb[g][:32, :], in_=skvd[g][:, 0, :]), g_w, reads=[skvB[g]], writes=[skb[g]], after=aft0)
                P.dma("gpsimd", lambda e, g=g: e.dma_start(out=vnew[g].ap.rearrange("p (h d) -> p h d", h=16)[:32, :, 0:64],
                                                            in_=skvd[g][:, 1, :].rearrange("p (h d) -> p h d", h=16)), g_w, reads=[skvB[g]], writes=[vnew[g]], after=aft0)
                V(lambda e, g=g: e.memset(vnew[g].ap.rearrange("p (h d) -> p h d", h=16)[:32, :, 64:65], 1.0), w=[vnew[g]])
            accP = [pb[4], pb[5], pb[6]]
            first = [True, True, True]
            units = [(0, 0, 1, list(range(8)))] + [(1, r, 4, [r, r + 4]) for r in range(4)] + [(2, s_, 16, [s_]) for s_ in range(8)]
            g_qb = [P.group("qb%d" % j) for j in range(4)]
            qbr = [tf[2], tf[3], tf[4], tf[5]]

            def p3s_units():
                qlist = []
                for b in range(4):
                    for u, (g, r0, dil, qs) in enumerate(units):
                        for s_ in qs:
                            qlist.append((b, u, g, s_))

                def issue_qb(n):
                    if n < len(qlist):
                        b_, u_, g_, s__ = qlist[n]
                        t_ = 8 * b_ + s__
                        P.dma("sync", lambda e, n=n, g_=g_, t_=t_: e.dma_start(out=qbr[n % 4][:, :], in_=sqd[g_][t_:t_ + 1, :].partition_broadcast(128)),
                              g_qb[n % 4], reads=[skvB[g_]], writes=[qbr[n % 4]])
                for n in range(3):
                    issue_qb(n)
                uc = 0
                qn = 0
                for b in range(4):
                    for u, (g, r0, dil, qs) in enumerate(units):
                        cb = ct[uc % 2]
                        gq = g_cache[uc % 2]
                        uc += 1
                        P.dma("sync", lambda e, g=g, b=b, r0=r0, dil=dil, cb=cb: e.dma_start(out=cb.ap.rearrange("p (a d) -> p a d", a=2),
                                                                                             in_=rows_ap(caches[g][b], r0, dil)), gq, writes=[cb])
                        A(lambda e, cb=cb: e.copy(out=vbf[:, :], in_=cb[:, 1024:2048]), [cb], [vbf])
                        for s_ in qs:
                            t = 8 * b + s_
                            par = qn % 2
                            qb = qbr[qn % 4]
                            issue_qb(qn + 3)
                            qn += 1
                            prod = tf[par]
                            sc_, pp_ = scs[par], pps[par]
                            pva = pvaug if par == 0 else pvaug2
                            V(lambda e, cb=cb, prod=prod, qb=qb: e.tensor_tensor(out=prod[:, :], in0=cb[:, 0:1024], in1=qb[:, :], op=ALU.mult), [cb, qb], [prod])
                            V(lambda e, prod=prod, sc_=sc_: e.tensor_reduce(out=sc_[:, 0:16], in_=prod.ap.rearrange("p (h d) -> p h d", h=16), axis=AX.X, op=ALU.add), [prod], [sc_])
                            A(lambda e, sc_=sc_: e.activation(out=sc_[:, 0:16], in_=sc_[:, 0:16], func=AF.Exp, scale=0.125), [sc_], [sc_])
                            so = (u * 8 + s_) * 16
                            V(lambda e, so=so, sc_=sc_, pp_=pp_: e.tensor_tensor(out=pp_[:, 0:16], in0=sc_[:, 0:16], in1=stabT[:, so:so + 16], op=ALU.mult), [sc_, stabT], [pp_])
                            pv3 = pva.ap.rearrange("p (h d) -> p h d", h=16)
                            G(lambda e, pv3=pv3, pp_=pp_: e.tensor_tensor(out=pv3[:, :, 0:64], in0=vbf.ap.rearrange("p (h d) -> p h d", h=16),
                                                                          in1=pp_[:, 0:16].unsqueeze(2).to_broadcast([128, 16, 64]), op=ALU.mult), [vbf, pp_], [pva])
                            G(lambda e, pv3=pv3, pp_=pp_: e.tensor_copy(out=pv3[:, :, 64:65], in_=pp_[:, 0:16].unsqueeze(2)), [pp_], [pva])
                            for k3, (c0, cw) in enumerate(((0, 455), (455, 455), (910, 130))):
                                T(lambda e, t=t, k3=k3, c0=c0, cw=cw, st_=first[k3], pva=pva: e.matmul(accP[k3][:32, 0:cw], lhsT=selk[:, t * 32:(t + 1) * 32], rhs=pva[:, c0:c0 + cw],
                                                                                                       start=st_, stop=False, skip_group_check=True), [selk, pva], [accP[k3]])
                                first[k3] = False
                            yield

            def p3s_tail():
                for g in range(3):
                    for src_, dsts in ((skb[g], [(KTn, 0, 128)]), (sqb[g], [(QTne, 0, 64), (QTno, 64, 128)])):
                        for c in range(8):
                            T(lambda e, c=c, src_=src_: e.transpose(out=pT[:, c * 32:(c + 1) * 32], in_=src_[:32, c * 128:(c + 1) * 128], identity=ident[:32, :32]),
                              [src_, ident], [pT])
                        for (d_, p0, p1) in dsts:
                            A(lambda e, d_=d_, p0=p0, p1=p1: e.copy(out=d_[p0:p1, :], in_=pT[p0:p1, 0:256]), [pT], [d_])
                    ps = pb[0]
                    for h in range(16):
                        QTx = QTne if h % 2 == 0 else QTno
                        c = h // 2
                        T(lambda e, h=h, c=c, QTx=QTx: e.matmul(ps[:32, h * 32:(h + 1) * 32], lhsT=KTn[:, c * 32:(c + 1) * 32], rhs=QTx[:, c * 32:(c + 1) * 32],
                                                                start=True, stop=True), [KTn, QTx], [ps])
                    A(lambda e: e.activation(out=En[:32, 0:512], in_=ps[:32, :], func=AF.Exp, scale=0.125), [ps], [En])
                    V(lambda e, g=g: e.tensor_tensor(out=Pn[:32, 0:512], in0=En[:32, 0:512], in1=sntab[:32, g * 512:(g + 1) * 512], op=ALU.mult), [En, sntab], [Pn])
                    for h in range(16):
                        k3 = h // 7
                        T(lambda e, h=h, k3=k3, g=g: e.matmul(accP[k3][:32, (h % 7) * 65:(h % 7) * 65 + 65], lhsT=Pn[:32, h * 32:(h + 1) * 32], rhs=vnew[g][:32, h * 65:(h + 1) * 65],
                                                              start=False, stop=(g == 2), skip_group_check=True), [Pn, vnew[g]], [accP[k3]])
                A(lambda e: e.copy(out=accst[:32, 0:455], in_=accP[0][:32, 0:455]), [accP[0]], [accst])
                V(lambda e: e.tensor_copy(out=accst[:32, 455:910], in_=accP[1][:32, 0:455]), [accP[1]], [accst])
                A(lambda e: e.copy(out=accst[:32, 910:1040], in_=accP[2][:32, 0:130]), [accP[2]], [accst])


        if stage >= 4:
            def final_tile(t, Tn, rows, ybufs, sample):
                i = t % 2
                xi = xin[i]
                P.dma("sync", lambda e: e.dma_start(out=xi[:Tn, :], in_=rows), g_in[i], reads=ybufs, writes=[xi])
                if not sample:
                    for gg in range(3):
                        P.dma("sync", lambda e, gg=gg: e.dma_start(out=accl[gg][:, :], in_=accd[gg][t * 128:(t + 1) * 128, :]), g_acc[gg], reads=[accb[gg]], writes=[accl[gg]] + vaug + KT + QTe + QTo)
                    V(lambda e: e.tensor_tensor(out=accl[0][:, :], in0=accl[0][:, :], in1=accl[1][:, :], op=ALU.add), [accl[0], accl[1]], [accl[0]])
                    V(lambda e: e.tensor_tensor(out=accl[0][:, :], in0=accl[0][:, :], in1=accl[2][:, :], op=ALU.add), [accl[0], accl[2]], [accl[0]])
                    ac = accl[0]
                else:
                    ac = accst
                yield
                a3 = ac.ap.rearrange("p (h d) -> p h d", h=16)[:Tn]
                sm = small[1]
                V(lambda e: e.reciprocal(out=sm[:Tn, :].unsqueeze(2), in_=a3[:, :, 64:65]), [ac], [sm])
                onb = tb[0]
                V(lambda e: e.tensor_tensor(out=onb.ap.rearrange("p (h d) -> p h d", h=16)[:Tn], in0=a3[:, :, 0:64],
                                            in1=sm[:Tn, :].unsqueeze(2).to_broadcast([Tn, 16, 64]), op=ALU.mult), [ac, sm], [onb])
                for c in range(8):
                    T(lambda e, c=c: e.transpose(out=pT[:, c * 128:c * 128 + Tn], in_=onb[:Tn, c * 128:(c + 1) * 128], identity=ident[:Tn, :Tn]), [onb, ident], [pT])
                yield
                oT_ = tb[1]
                A(lambda e: e.copy(out=oT_.ap.rearrange("p (c t) -> p c t", c=8)[:, :, :Tn], in_=pT.ap.rearrange("p (c t) -> p c t", c=8)[:, :, :Tn]), [pT], [oT_])
                yield
                for half in range(2):
                    for c in range(8):
                        T(lambda e, half=half, c=c: e.matmul(pb[half][:Tn, :], lhsT=oT_[:, c * 128:c * 128 + Tn], rhs=Wao[:, c, half * 512:(half + 1) * 512],
                                                             start=c == 0, stop=c == 7), [oT_, arenaB, WB[4]], [pb[half]])
                yield
                yb = yo[i]
                for half in range(2):
                    V(lambda e, half=half: e.tensor_tensor(out=yb[:Tn, half * 512:(half + 1) * 512], in0=xi[:Tn, half * 512:(half + 1) * 512],
                                                           in1=pb[half][:Tn, :], op=ALU.add), [xi, pb[half]], [yb])
                store_rows(rows, yb, Tn, i, ybufs)

            gu = p3s_units()
            for t in range(16):
                gf = final_tile(t, 128, ybuf[2048 + t * 128:2048 + (t + 1) * 128, :], [ytile[16 + t]], False)
                nq = 0
                for _ in gf:
                    next(gu, None)
                    nq += 1
                while nq < 6:
                    next(gu, None)
                    nq += 1
            for _ in gu:
                pass
            p3s_tail()
            for _ in final_tile(0, 32, ybuf[4096:4128, :], [ytile[32]], True):
                pass
            allov2 = p3bufs
            for kc in range(8):
                for c0 in range(0, 4096, 2048):
                    P.dma("gpsimd", lambda e, kc=kc, c0=c0: e.dma_start(out=Wu[:, kc, c0:c0 + 2048], in_=wup[1, kc * 128:(kc + 1) * 128, c0:c0 + 2048]),
                          GW[0], writes=[WB[0]], after=[arenaB] + allov2)
            for kc in range(32):
                P.dma("gpsimd", lambda e, kc=kc: e.dma_start(out=Wd[:, kc, :], in_=wdn[1, kc * 128:(kc + 1) * 128, :]), GW[1], writes=[WB[1]], after=[arenaB] + allov2)
            load_norm_w(0, nffn[1:2, :])
            jobs = [(ybuf[2048 + t * 128:2048 + (t + 1) * 128, :], [ytile[16 + t]], yp[t * 128:(t + 1) * 128, :], [], 128) for t in range(16)]
            jobs.append((ybuf[4096:4128, :], [ytile[32]], ys[:, :], [], 32))
            mlp_pass(jobs)
        P.emit()
    return nc


def _consts():
    c = {}
    c["idn"] = np.eye(128, dtype=np.float32)
    s = np.arange(128)
    c["hmask"] = ((s[:, None] // 64 == s[None, :] // 64) & (s[:, None] <= s[None, :])).astype(np.float32)
    s8 = np.arange(32)
    c["hmask8"] = ((s8[:, None] // 8 == s8[None, :] // 8) & (s8[:, None] <= s8[None, :])).astype(np.float32)
    t = np.arange(1024) % 128
    c["rst"] = np.stack([(t % 64 != 0), (t % 8 != 0)]).astype(np.float32)
    n = 48
    slopes = (2.0 ** (-8.0 * np.arange(1, n + 1) / n)).reshape(3, 16)
    k = np.arange(128)[:, None]
    q = np.arange(128)[None, :]
    mt = np.zeros((128, 3, 16, 2, 128), np.float32)
    for g, (w, d) in enumerate(GROUPS):
        for h in range(16):
            cgh = slopes[g, h] * d
            dist_prev = q - k + 128
            mt[:, g, h, 0, :] = np.where(k >= q, np.exp(-cgh * dist_prev), 0.0)
            dist_cur = q - k
            mt[:, g, h, 1, :] = np.where(k <= q, np.exp(-cgh * np.maximum(dist_cur, 0)), 0.0)
    c["mtab"] = mt.reshape(128, 48 * 256)
    bm = np.zeros((32, 4), np.float32)
    for b in range(4):
        bm[8 * b:8 * b + 8, b] = 1.0
    c["bmask"] = bm
    stab = np.zeros((128, 13, 8, 16), np.float64)
    i_ = np.arange(128)[:, None]
    for s_ in range(8):
        stab[:, 0, s_, :] = np.where(i_ >= s_, np.exp(-slopes[0][None, :] * (128 + s_ - i_)), 0.0)
    for r in range(4):
        for m, s_ in enumerate((r, r + 4)):
            j = 128 + m - i_
            stab[:, 1 + r, s_, :] = np.where(i_ >= m, np.exp(-slopes[1][None, :] * 4.0 * j), 0.0)
    for s_ in range(8):
        stab[:, 5 + s_, s_, :] = np.exp(-slopes[2][None, :] * 16.0 * (128 - i_))
    c["stab"] = stab.reshape(128, 13 * 8 * 16).astype(np.float32)
    sn = np.zeros((32, 3, 16, 32), np.float64)
    for g, (w, d) in enumerate(GROUPS):
        for kb in range(4):
            for n_ in range(8):
                for s_ in range(8):
                    dd = s_ - n_
                    if dd >= 0 and dd % d == 0:
                        sn[kb * 8 + n_, g, :, kb * 8 + s_] = np.exp(-slopes[g] * dd)
    c["sntab"] = sn.reshape(32, 3 * 16 * 32).astype(np.float32)
    sq_ = np.zeros((32, 32, 128), np.float32)
    sk_ = np.zeros((128, 32, 32), np.float32)
    for t in range(32):
        sq_[t, t, :] = 1.0
        sk_[:, t, t] = 1.0
    c["selq"] = sq_.reshape(32, 32 * 128)
    c["selk"] = sk_.reshape(128, 32 * 32)
    return c


_NC_CACHE = {}


def kernel(x_prompt, x_sample, state_hgrn, cache_kv_w128, cache_kv_w512, cache_kv_w2048,
           hg_lb_logits, hg_w_q, hg_w_f, hg_w_i, hg_w_g, hg_w_o, hg_norm_o,
           att_w_qkv, att_w_o, att_q_norm, att_k_norm,
           norm_mix, norm_ffn, ffn_w_up, ffn_w_down, _stage=99, _debug=False):
    f = lambda a: np.ascontiguousarray(np.asarray(a, dtype=np.float32))
    x_prompt, x_sample, state_hgrn = f(x_prompt), f(x_sample), f(state_hgrn)
    cs = [f(cache_kv_w128), f(cache_kv_w512), f(cache_kv_w2048)]
    shared = {
        "lbl": f(hg_lb_logits), "hg_q": f(hg_w_q)[0], "hg_f": f(hg_w_f)[0], "hg_i": f(hg_w_i)[0], "hg_g": f(hg_w_g)[0],
        "hg_o": f(hg_w_o)[0], "gno": f(hg_norm_o), "wqkv": f(att_w_qkv)[0], "awo": f(att_w_o)[0], "qn_w": f(att_q_norm),
        "kn_w": f(att_k_norm), "nmix": f(norm_mix), "nffn": f(norm_ffn), "wup": f(ffn_w_up), "wdn": f(ffn_w_down),
    }
    shared.update(_consts())
    in_maps = []
    for c in range(8):
        b, half = c // 2, c % 2
        m = dict(shared)
        if half == 0:
            xs = np.concatenate([np.zeros((2048, 1024), np.float32), x_prompt[b, :2048]], 0)
        else:
            xs = x_prompt[b]
        m["xs"] = np.ascontiguousarray(xs)
        m["xsm"] = np.ascontiguousarray(x_sample[4 * c:4 * c + 4].reshape(32, 1024))
        m["st_in"] = np.ascontiguousarray(state_hgrn[0, 4 * c:4 * c + 4])
        for g in range(3):
            m["c%d" % g] = np.ascontiguousarray(cs[g][0, 4 * c:4 * c + 4].reshape(4, GROUPS[g][0], 2, 1024))
        m["flag"] = np.full((128, 1), float(half), np.float32)
        in_maps.append(m)
    if _stage not in _NC_CACHE:
        _NC_CACHE[_stage] = build_nc(_stage)
    nc = _NC_CACHE[_stage]
    res = run_bass_kernel_spmd(nc, in_maps, core_ids=list(range(8)))
    R = res.results
    if _debug:
        return R
    y_prompt = np.zeros((4, 4096, 1024), np.float32)
    y_sample = np.zeros((32, 8, 1024), np.float32)
    st_p = np.zeros((1, 4, 8, 128, 128), np.float32)
    st_s = np.zeros((1, 32, 8, 128, 128), np.float32)
    kvp = [np.zeros((1, 4, GROUPS[g][0], 2, 16, 64), np.float32) for g in range(3)]
    kvs = [np.zeros((1, 32, 8, 2, 16, 64), np.float32) for g in range(3)]
    for c in range(8):
        b, half = c // 2, c % 2
        y_prompt[b, half * 2048:(half + 1) * 2048] = R[c]["yp"]
        y_sample[4 * c:4 * c + 4] = R[c]["ys"].reshape(4, 8, 1024)
        st_s[0, 4 * c:4 * c + 4] = R[c]["st_s"]
        for g in range(3):
            kvs[g][0, 4 * c:4 * c + 4] = R[c]["kvs%d" % g].reshape(4, 8, 2, 16, 64)
        if half == 1:
            st_p[0, b] = R[c]["st_p"]
            for g in range(3):
                kvp[g][0, b] = R[c]["kvp%d" % g].reshape(GROUPS[g][0], 2, 16, 64)
    return (y_prompt, y_sample, st_p, st_s, kvp[0], kvs[0], kvp[1], kvs[1], kvp[2], kvs[2])
```
